# Optimizing a Trainium2 kernel written in Bass

```python
import math
import jax, jax.numpy as jnp
from jax import lax
import numpy as np

D_MODEL = 2048
BATCH = 4
SEQ = 4096
DEPTH = 2

CHUNK = 64
QB = 128
HEAD_DIM = 64
A_HEADS = 16
A_KV_HEADS = 4
WINDOW = 128
WINDOW_CHUNKS = WINDOW // CHUNK
B_HEADS = 16
KV_RANK = 256
IDX_HEADS = 8
IDX_DIM = 64
TOPK_MAX = 256
C_HEADS = 16
C_HEAD_DIM = 128
NUM_BUCKETS = 32
MAX_DISTANCE = 128
D_FF = 5632
CONV_W = 3
EPS = 1e-6

N_EVEN = (DEPTH + 1) // 2
N_ODD = DEPTH // 2
EVEN_SPLITS = (A_HEADS * HEAD_DIM, A_KV_HEADS * HEAD_DIM, A_KV_HEADS * HEAD_DIM,
               B_HEADS * HEAD_DIM, KV_RANK, IDX_HEADS * IDX_DIM, IDX_DIM, IDX_HEADS)
EVEN_IN = sum(EVEN_SPLITS)
MIX_WIDTH_EVEN = (A_HEADS + B_HEADS) * HEAD_DIM
MIX_WIDTH_ODD = C_HEADS * C_HEAD_DIM

kernel_name = 'hybrid_chunk_causal_swa_dsa_stickbreak_convffn'


def rms_norm(x, g):
    xf = x.astype(jnp.float32)
    y = xf * lax.rsqrt(jnp.mean(xf * xf, axis=-1, keepdims=True) + EPS)
    return (y * g.astype(jnp.float32)).astype(x.dtype)


def split_cols(a, sizes):
    offs = np.cumsum(sizes)[:-1].tolist()
    return jnp.split(a, offs, axis=-1)


def t5_bucket(rel):
    half = NUM_BUCKETS // 2
    max_exact = half // 2
    n = jnp.abs(rel)
    nf = jnp.maximum(n, 1).astype(jnp.float32)
    large = max_exact + (jnp.log(nf / max_exact) / math.log(MAX_DISTANCE / max_exact)
                         * (half - max_exact)).astype(jnp.int32)
    large = jnp.minimum(large, half - 1)
    return jnp.where(rel > 0, half, 0) + jnp.where(n < max_exact, n, large)


def to_blocks(a):
    return a.reshape(a.shape[0], a.shape[1] // QB, QB, *a.shape[2:]).swapaxes(0, 1)


def from_blocks(o):
    return o.swapaxes(0, 1).reshape(o.shape[1], o.shape[0] * QB, -1)


def swa_sink_attention(q, k, v, sinks, bias_table):
    Bn, S, H, Dh = q.shape
    KV = k.shape[2]
    G = H // KV
    nblk = S // QB

    def band(a):
        ap = jnp.pad(a, ((0, 0), (QB, 0), (0, 0), (0, 0))).reshape(Bn, nblk + 1, QB, KV, Dh)
        return jnp.concatenate([ap[:, :-1], ap[:, 1:]], axis=2)

    kb, vb = band(k), band(v)
    qg = q.reshape(Bn, nblk, QB, KV, G, Dh)
    s = jnp.einsum('bnqkgd,bnskd->bnkgqs', qg, kb).astype(jnp.float32) * Dh ** -0.5
    q_off = jnp.arange(QB)
    k_off = jnp.arange(2 * QB) - QB
    rel = k_off[None, :] - q_off[:, None]
    bias = bias_table.astype(jnp.float32)[t5_bucket(rel)]
    bias = jnp.transpose(bias, (2, 0, 1)).reshape(KV, G, QB, 2 * QB)
    start = jnp.arange(nblk)[:, None] * QB
    qpos = start + q_off
    kpos = start + k_off
    qc = qpos // CHUNK
    kc = jnp.floor_divide(kpos, CHUNK)
    allowed = ((kpos >= 0)[:, None, :]
               & (kc[:, None, :] <= qc[:, :, None])
               & (kc[:, None, :] >= qc[:, :, None] - WINDOW_CHUNKS))
    s = jnp.where(allowed[None, :, None, None], s + bias, -jnp.inf)
    sink = sinks.astype(jnp.float32).reshape(KV, G)[:, :, None, None]
    m = jnp.maximum(jnp.max(s, axis=-1, keepdims=True), sink)
    p = jnp.exp(s - m)
    p = p / (jnp.sum(p, axis=-1, keepdims=True) + jnp.exp(sink - m))
    o = jnp.einsum('bnkgqs,bnskd->bnqkgd', p.astype(v.dtype), vb)
    return o.reshape(Bn, S, H * Dh)


def dsa_attention(q, c_kv, q_idx, k_idx, w_idx, w_uk, w_uv, bias_table):
    Bn, S, H, Dh = q.shape
    topk = min(TOPK_MAX, S // 4)
    kchunk = jnp.arange(S) // CHUNK
    bias_table = bias_table.astype(jnp.float32)

    def block(args):
        qb, qib, wb, n = args
        qpos = n * QB + jnp.arange(QB)
        qchunk = qpos // CHUNK
        r = jax.nn.relu(jnp.einsum('bqhd,bsd->bqhs', qib, k_idx).astype(jnp.float32) * IDX_DIM ** -0.5)
        score = jnp.einsum('bqhs,bqh->bqs', r, wb.astype(jnp.float32) * IDX_HEADS ** -0.5)
        score = jnp.where((kchunk[None, :] <= qchunk[:, None])[None], score, -jnp.inf)
        _, idx = lax.top_k(score, topk)
        valid = (idx // CHUNK) <= qchunk[None, :, None]
        c_sel = jax.vmap(lambda cb, ib: cb[ib])(c_kv, idx)
        q_lat = jnp.einsum('bqhd,hdr->bqhr', qb, w_uk)
        s = jnp.einsum('bqhr,bqkr->bqhk', q_lat, c_sel).astype(jnp.float32) * Dh ** -0.5
        bias = bias_table[t5_bucket(idx - qpos[None, :, None])]
        s = jnp.where(valid[:, :, None, :], s + jnp.swapaxes(bias, 2, 3), -jnp.inf)
        p = jax.nn.softmax(s, axis=-1).astype(c_sel.dtype)
        o_lat = jnp.einsum('bqhk,bqkr->bqhr', p, c_sel)
        return jnp.einsum('bqhr,hrd->bqhd', o_lat, w_uv)

    out = lax.map(block, (to_blocks(q), to_blocks(q_idx), to_blocks(w_idx), jnp.arange(S // QB)))
    return from_blocks(out)


def stick_breaking_attention(q, k, v):
    Bn, S, H, Dh = q.shape
    kpos = jnp.arange(S)

    def block(args):
        qb, n = args
        qpos = n * QB + jnp.arange(QB)
        z = jnp.einsum('bqhd,bshd->bhqs', qb, k).astype(jnp.float32) * Dh ** -0.5
        earlier = kpos[None, :] < qpos[:, None]
        log_keep = jnp.where(earlier, jax.nn.log_sigmoid(-z), 0.0)
        between = lax.cumsum(log_keep, axis=3, reverse=True) - log_keep
        a = jnp.where(earlier, jnp.exp(jax.nn.log_sigmoid(z) + between), 0.0)
        return jnp.einsum('bhqs,bshd->bqhd', a.astype(v.dtype), v)

    return from_blocks(lax.map(block, (to_blocks(q), jnp.arange(S // QB))))


def even_mixer(h, w_in, kv_norm_g, w_uk, w_uv, sinks, w_out, rel_bias):
    Bn, S, _ = h.shape
    qa, ka, va, qb, c_lat, qi, ki, wi = split_cols(h @ w_in, EVEN_SPLITS)
    o_a = swa_sink_attention(qa.reshape(Bn, S, A_HEADS, HEAD_DIM),
                             ka.reshape(Bn, S, A_KV_HEADS, HEAD_DIM),
                             va.reshape(Bn, S, A_KV_HEADS, HEAD_DIM),
                             sinks, rel_bias[:, :A_HEADS])
    o_b = dsa_attention(qb.reshape(Bn, S, B_HEADS, HEAD_DIM), rms_norm(c_lat, kv_norm_g),
                        qi.reshape(Bn, S, IDX_HEADS, IDX_DIM), ki, wi, w_uk, w_uv,
                        rel_bias[:, A_HEADS:])
    return jnp.concatenate([o_a, o_b], axis=-1) @ w_out


def odd_mixer(h, w_in, w_out):
    Bn, S, _ = h.shape
    q, k, v = jnp.split(h @ w_in, 3, axis=-1)
    shp = (Bn, S, C_HEADS, C_HEAD_DIM)
    return stick_breaking_attention(q.reshape(shp), k.reshape(shp), v.reshape(shp)) @ w_out


def conv_ffn(h, w_up, conv_w, conv_b, w_down):
    S = h.shape[1]
    u = h @ w_up
    up = jnp.pad(u, ((0, 0), (CONV_W - 1, 0), (0, 0)))
    u = sum(conv_w[j] * up[:, j:j + S] for j in range(CONV_W)) + conv_b
    gate, val = jnp.split(u, 2, axis=-1)
    return (jax.nn.silu(gate) * val) @ w_down


def setup_inputs(seed: int = 0) -> dict:
    key = jax.random.key(seed)
    ks = jax.random.split(key, 21)
    D = D_MODEL

    def nrm(k, shape, scale):
        return jax.random.normal(k, shape, jnp.float32) * scale

    return {
        'x': nrm(ks[0], (BATCH, SEQ, D), 1.0),
        'c': nrm(ks[1], (BATCH, D), 1.0),
        'rel_bias': nrm(ks[2], (NUM_BUCKETS, A_HEADS + B_HEADS), 0.5),
        'ada_w': nrm(ks[3], (DEPTH, D, 6 * D), D ** -0.5),
        'ada_b': nrm(ks[4], (DEPTH, 6 * D), 0.02),
        'norm_mix_g': 1.0 + nrm(ks[5], (DEPTH, D), 0.02),
        'norm_ffn_g': 1.0 + nrm(ks[6], (DEPTH, D), 0.02),
        'ev_w_in': nrm(ks[7], (N_EVEN, D, EVEN_IN), D ** -0.5),
        'ev_kv_norm_g': 1.0 + nrm(ks[8], (N_EVEN, KV_RANK), 0.02),
        'ev_w_uk': nrm(ks[9], (N_EVEN, B_HEADS, HEAD_DIM, KV_RANK), KV_RANK ** -0.5),
        'ev_w_uv': nrm(ks[10], (N_EVEN, B_HEADS, KV_RANK, HEAD_DIM), KV_RANK ** -0.5),
        'ev_sinks': nrm(ks[11], (N_EVEN, A_HEADS), 0.5),
        'ev_w_out': nrm(ks[12], (N_EVEN, MIX_WIDTH_EVEN, D), MIX_WIDTH_EVEN ** -0.5),
        'od_w_in': nrm(ks[13], (N_ODD, D, 3 * MIX_WIDTH_ODD), D ** -0.5),
        'od_w_out': nrm(ks[14], (N_ODD, MIX_WIDTH_ODD, D), MIX_WIDTH_ODD ** -0.5),
        'ffn_w_up': nrm(ks[15], (DEPTH, D, 2 * D_FF), D ** -0.5),
        'ffn_conv_w': nrm(ks[16], (DEPTH, CONV_W, 2 * D_FF), CONV_W ** -0.5),
        'ffn_conv_b': nrm(ks[17], (DEPTH, 2 * D_FF), 0.02),
        'ffn_w_down': nrm(ks[18], (DEPTH, D_FF, D), D_FF ** -0.5),
        'final_g': 1.0 + nrm(ks[19], (D,), 0.02),
    }


def reference(x, c, rel_bias, ada_w, ada_b, norm_mix_g, norm_ffn_g, ev_w_in, ev_kv_norm_g,
              ev_w_uk, ev_w_uv, ev_sinks, ev_w_out, od_w_in, od_w_out, ffn_w_up, ffn_conv_w,
              ffn_conv_b, ffn_w_down, final_g):
    c_act = jax.nn.silu(c)
    for i in range(DEPTH):
        mod = c_act @ ada_w[i] + ada_b[i]
        sh1, sc1, g1, sh2, sc2, g2 = [m[:, None, :] for m in jnp.split(mod, 6, axis=-1)]
        h = rms_norm(x, norm_mix_g[i]) * (1.0 + sc1) + sh1
        if i % 2 == 0:
            j = i // 2
            y = even_mixer(h, ev_w_in[j], ev_kv_norm_g[j], ev_w_uk[j], ev_w_uv[j],
                           ev_sinks[j], ev_w_out[j], rel_bias)
        else:
            j = i // 2
            y = odd_mixer(h, od_w_in[j], od_w_out[j])
        x = x + g1 * y
        h = rms_norm(x, norm_ffn_g[i]) * (1.0 + sc2) + sh2
        x = x + g2 * conv_ffn(h, ffn_w_up[i], ffn_conv_w[i], ffn_conv_b[i], ffn_w_down[i])
    return rms_norm(x, final_g)
```

```python
from contextlib import ExitStack
import numpy as np
import concourse.bass as bass
import concourse.mybir as mybir
from concourse.bass_utils import run_bass_kernel_spmd

F32 = mybir.dt.float32
BF16 = mybir.dt.bfloat16
AF = mybir.ActivationFunctionType
ALU = mybir.AluOpType
AX = mybir.AxisListType


class Buf:
    def __init__(self, name, t):
        self.name = name
        self.t = t

    def __getitem__(self, idx):
        return self.t[idx]


class Sched:
    ENG = ("pe", "act", "dve", "pool", "sp")
    NRING = {"sp": ("sp", 12), "pool": ("pool", 8), "cast": ("pool", 16)}

    def __init__(self, nc, same_engine_sync=True):
        self.nc = nc
        self.stack = ExitStack()
        self.same_engine_sync = same_engine_sync
        self.streams = {e: [] for e in self.ENG}
        self.cnt = {}
        self.esem = {}
        for e in ("pe", "act", "dve", "pool"):
            self.esem[e] = self.stack.enter_context(nc.semaphore("es_" + e))
            self.cnt[e] = 0
        self.ring = {}
        self.ring_tot = {}
        self.ring_i = {}
        for q, (_e, n) in self.NRING.items():
            self.ring[q] = [self.stack.enter_context(nc.semaphore(f"dq_{q}_{i}")) for i in range(n)]
            self.ring_tot[q] = [0] * n
            self.ring_i[q] = 0
        self.known = {e: {} for e in self.ENG}
        self.last_w = {}
        self.readers = {}
        self.semname = {}
        self.n_ops = 0

    def sbuf(self, name, shape, dtype):
        self.uid = getattr(self, "uid", 0) + 1
        name = f"{name}_{self.uid}"
        t = self.stack.enter_context(self.nc.sbuf_tensor(name, list(shape), dtype))
        return Buf(name, t)

    def psum(self, name, shape, dtype):
        self.uid = getattr(self, "uid", 0) + 1
        name = f"{name}_{self.uid}"
        t = self.stack.enter_context(self.nc.psum_tensor(name, list(shape), dtype))
        return Buf(name, t)

    @staticmethod
    def _key(r):
        return r if isinstance(r, (str, tuple)) else id(r)

    def _deps(self, eng, reads, writes):
        toks = []
        for r in reads:
            k = self._key(r)
            if k in self.last_w:
                toks.append(self.last_w[k])
        for w in writes:
            k = self._key(w)
            if k in self.last_w:
                toks.append(self.last_w[k])
            for sem_id, tok in self.readers.get(k, {}).items():
                toks.append(tok)
        best = {}
        for sem, v in toks:
            sid = id(sem)
            if sid not in best or best[sid][1] < v:
                best[sid] = (sem, v)
        waits = []
        own = self.esem.get(eng)
        for sid, (sem, v) in best.items():
            if own is not None and sem is own:
                if eng == "pe" or not self.same_engine_sync:
                    continue
            if self.known[eng].get(sid, 0) >= v:
                continue
            self.known[eng][sid] = v
            waits.append((sem, v))
        return waits

    def _commit(self, tok, reads, writes):
        for r in reads:
            k = self._key(r)
            d = self.readers.setdefault(k, {})
            sid = id(tok[0])
            if sid not in d or d[sid][1] < tok[1]:
                d[sid] = tok
        for w in writes:
            k = self._key(w)
            self.last_w[k] = tok
            self.readers[k] = {}

    def op(self, eng, fn, reads=(), writes=()):
        waits = self._deps(eng, reads, writes)
        self.cnt[eng] += 1
        tok = (self.esem[eng], self.cnt[eng])
        self.streams[eng].append((waits, fn, (self.esem[eng], 1)))
        self._commit(tok, reads, writes)
        self.n_ops += 1
        return tok

    def dma(self, q, out, in_, reads=(), writes=(), **kw):
        i = self.ring_i[q]
        self.ring_i[q] = (i + 1) % len(self.ring[q])
        sem = self.ring[q][i]
        prev = self.ring_tot[q][i]
        eng = self.NRING[q][0]
        waits = self._deps(eng, reads, writes)
        if prev > 0 and self.known[eng].get(id(sem), 0) < prev:
            self.known[eng][id(sem)] = prev
            waits.append((sem, prev))
        self.ring_tot[q][i] = prev + 16
        tok = (sem, prev + 16)
        self.streams[eng].append((waits, (lambda e: e.dma_start(out=out, in_=in_, **kw)), (sem, 16)))
        self._commit(tok, reads, writes)
        self.n_ops += 1
        return tok

    def dma_like(self, q, fn, reads=(), writes=()):
        i = self.ring_i[q]
        self.ring_i[q] = (i + 1) % len(self.ring[q])
        sem = self.ring[q][i]
        prev = self.ring_tot[q][i]
        eng = self.NRING[q][0]
        waits = self._deps(eng, reads, writes)
        if prev > 0 and self.known[eng].get(id(sem), 0) < prev:
            self.known[eng][id(sem)] = prev
            waits.append((sem, prev))
        self.ring_tot[q][i] = prev + 16
        tok = (sem, prev + 16)
        self.streams[eng].append((waits, fn, (sem, 16)))
        self._commit(tok, reads, writes)
        return tok

    def barrier(self):
        toks = [(self.esem[e], self.cnt[e]) for e in self.esem if self.cnt[e] > 0]
        for q in ("sp", "pool"):
            for sem, tot in zip(self.ring[q], self.ring_tot[q]):
                if tot > 0:
                    toks.append((sem, tot))
        for eng in self.ENG:
            waits = []
            for sem, v in toks:
                if sem is self.esem.get(eng):
                    continue
                if self.known[eng].get(id(sem), 0) >= v:
                    continue
                self.known[eng][id(sem)] = v
                waits.append((sem, v))
            if waits:
                self.streams[eng].append((waits, None, None))

    def finish(self, outputs):
        nc = self.nc
        final_waits = []
        for o in outputs:
            k = self._key(o)
            if k in self.last_w:
                final_waits.append(self.last_w[k])
        streams = self.streams
        engmap = {"pe": "tensor", "act": "scalar", "dve": "vector", "pool": "gpsimd", "sp": "sync"}

        def replay(name, e):
            for waits, fn, inc in streams[name]:
                for sem, v in waits:
                    e.wait_ge(sem, v)
                if fn is None:
                    continue
                ins = fn(e)
                ins.then_inc(inc[0], inc[1])
            if name == "sp":
                for sem, v in final_waits:
                    e.wait_ge(sem, v)

        with nc.Block() as block:
            for name in self.ENG:
                if not streams[name] and name != "sp":
                    continue
                getattr(block, engmap[name])(lambda e, name=name: replay(name, e))
        self.stack.close()


D = 2048
KC = 16
T = 4096
TT = 512
NT = T // TT
NB = T // 128
DFF = 5632
NFC = 88
EPS = 1e-6
NEG = -30000.0
NEGM_DT = mybir.dt.bfloat16
NEGM_V = -30000.0
N_CORES = 8


def _t5_bucket_np(rel):
    half, max_exact = 16, 8
    n = np.abs(rel)
    nf = np.maximum(n, 1).astype(np.float32)
    large = max_exact + (np.log(nf / max_exact) / np.log(np.float32(128 / max_exact)) * (half - max_exact)).astype(np.int32)
    large = np.minimum(large, half - 1)
    return np.where(rel > 0, half, 0) + np.where(n < max_exact, n, large)


def host_constants():
    c = {}
    c["ident"] = np.eye(128, dtype=np.float32)
    j = np.arange(128)
    c["negU"] = -(j[:, None] >= j[None, :]).astype(np.float32)
    c["Jmat"] = np.eye(128, dtype=np.float32)[::-1].copy()
    s = np.arange(128)[:, None]
    t = np.arange(512)[None, :]
    sb01 = np.zeros((128, 4, 512), np.float32)
    for r in range(4):
        sb01[:, r, :] = ((128 * r + s) < t)
    c["sb01"] = sb01.reshape(128, 2048)
    c["sbneg"] = ((1.0 - sb01) * NEG).reshape(128, 2048).astype(np.float32)
    tq = np.arange(128)[:, None]
    sk = np.arange(128)[None, :]
    c["idxdiag"] = np.where((sk // 64) <= (tq // 64), 0.0, -1e30).astype(np.float32)
    sp_ = np.arange(128)[:, None]
    tt_ = np.arange(128)[None, :]
    sw = np.zeros((128, 2, 128), np.float32)
    for di, delta in enumerate((0, -1)):
        kpos = 128 * (1 + delta) + (127 - sp_)
        qpos = 128 + tt_
        kc_, qc_ = kpos // 64, qpos // 64
        ok = (kc_ <= qc_) & (kc_ >= qc_ - 2)
        sw[:, di, :] = np.where(ok, 0.0, NEG)
    c["swamask"] = sw.reshape(128, 256)
    x = np.arange(384)
    relpos = 127 - x
    b = _t5_bucket_np(relpos.astype(np.int32))
    oh = np.zeros((32, 384), np.float32)
    oh[b, x] = 1.0
    oh[:, 383] = 0.0
    c["onehotR"] = oh
    k = np.arange(20, dtype=np.float64)
    c["pow2a"] = np.broadcast_to((2.0 ** -k)[None, :], (128, 20)).astype(np.float32).copy()
    c["pow2b"] = np.broadcast_to((2.0 ** -(k + 1))[None, :], (128, 20)).astype(np.float32).copy()
    return c


def build_program(debug=(), upto=99):
    nc = bass.Bass("TRN2", target_bir_lowering=False)
    S = Sched(nc)
    debug = set(debug)

    def din(name, shape, dt=F32):
        return nc.dram_tensor(name, list(shape), dt, kind="ExternalInput").ap()

    def dscr(name, shape, dt):
        kind = "ExternalOutput" if name in debug else "Internal"
        return nc.dram_tensor(name, list(shape), dt, kind=kind).ap()

    x_in = din("x", [T, D])
    cT_in = din("cT", [128, KC])
    relb_in = din("rel_bias", [32, 32])
    adaw_in = din("ada_w", [2, D, 6 * D])
    adab_in = din("ada_bT", [128, 192])
    gmix_in = din("gmixT", [128, 32])
    gffn_in = din("gffnT", [128, 32])
    evwin_in = din("ev_w_in", [D, 3400])
    kvg_in = din("kvg_bc", [128, 256])
    wuk_in = din("ev_w_uk", [16, 64, 256])
    wuv_in = din("ev_w_uv", [16, 256, 64])
    sinks_in = din("sinks_bc", [128, 16])
    evwout_in = din("ev_w_out", [D, D])
    odwin_in = din("od_w_in", [D, 3 * D])
    odwout_in = din("od_w_out", [D, D])
    wup_in = din("ffn_w_up", [2, D, 2 * DFF])
    convw_in = din("convwT", [128, 2 * 3 * NFC])
    convb_in = din("convbT", [128, 2 * NFC])
    wdn_in = din("ffn_w_down", [2, DFF, D])
    fing_in = din("finalgT", [128, KC])
    cst = {k: din("c_" + k, list(v.shape)) for k, v in host_constants().items()}
    out_ap = nc.dram_tensor("out", [T, D], F32, kind="ExternalOutput").ap()

    def MM(out, lhsT, rhs, start, stop, R, W):
        S.op("pe", lambda e: e.matmul(out, lhsT, rhs, start=start, stop=stop), R, W)

    def TR(out, in_, ident, R, W):
        S.op("pe", lambda e: e.transpose(out, in_, ident), R, W)

    def ACT(out, in_, func, R, W, bias=0.0, scale=1.0, accum=None):
        if accum is None:
            S.op("act", lambda e: e.activation(out, in_, func, bias=bias, scale=scale), R, W)
        else:
            S.op("act", lambda e: e.activation(out, in_, func, bias=bias, scale=scale, accum_out=accum), R, W)

    def TS(eng, out, in0, s1, s2, op0, op1, R, W):
        if s2 is None:
            S.op(eng, lambda e: e.tensor_scalar(out, in0, s1, None, op0), R, W)
        else:
            S.op(eng, lambda e: e.tensor_scalar(out, in0, s1, s2, op0, op1), R, W)

    def TTo(eng, out, in0, in1, op, R, W):
        S.op(eng, lambda e: e.tensor_tensor(out, in0, in1, op), R, W)

    def STT(eng, out, in0, sc, in1, op0, op1, R, W):
        S.op(eng, lambda e: e.scalar_tensor_tensor(out, in0, sc, in1, op0, op1), R, W)

    def CP(eng, out, in_, R, W):
        if eng == "act":
            S.op("act", lambda e: e.copy(out, in_), R, W)
        else:
            S.op(eng, lambda e: e.tensor_copy(out, in_), R, W)

    def MEMSET(eng, ap, val, W):
        S.op(eng, lambda e: e.memset(ap, val), (), W)

    def RECIP(out, in_, R, W):
        S.op("dve", lambda e: e.reciprocal(out, in_), R, W)

    def DMA(q, out, in_, R, W, **kw):
        S.dma(q, out, in_, R, W, **kw)

    evac_i = [0]

    def EVAC(out, in_, R, W, scale=None):
        evac_i[0] += 1
        if evac_i[0] % 2 == 0:
            if scale is None:
                CP("act", out, in_, R, W)
            else:
                ACT(out, in_, AF.Copy, R, W, scale=scale)
        else:
            if scale is None:
                CP("dve", out, in_, R, W)
            else:
                TS("dve", out, in_, scale, None, ALU.mult, None, R, W)

    def wt(name, nchunks, kc):
        return dscr(name, [nchunks, 128, kc, 128], BF16)

    W_evfm = wt("W_evfm", 25, KC)
    W_evtm = wt("W_evtm", 7, KC)
    W_evout = wt("W_evout", 16, KC)
    W_odin = wt("W_odin", 48, KC)
    W_odout = wt("W_odout", 16, KC)
    W_up = [wt(f"W_up{l}", NFC, KC) for l in range(2)]
    W_dn = [wt(f"W_dn{l}", 16, 44) for l in range(2)]
    xT = dscr("xT", [D, T], F32)
    qAT = dscr("qAT", [1024, T], BF16)
    kA2T = dscr("kA2T", [512, T], BF16)
    vA2 = dscr("vA2", [T, 512], BF16)
    qBT = dscr("qBT", [1024, T], BF16)
    ckv = dscr("ckv", [T, 256], BF16)
    ckvT = dscr("ckvT", [256, T], BF16)
    qiT = dscr("qiT", [512, T], BF16)
    kiT2 = dscr("kiT2", [128, T], BF16)
    wi_d = dscr("wi_d", [T, 8], F32)
    oT = dscr("oT", [D, T], BF16)
    q1T = dscr("q1T", [D, T], BF16)
    k1T = dscr("k1T", [D, T], BF16)
    v1 = dscr("v1", [T, D], BF16)
    gvec = dscr("gvec", [32, 384], F32)

    def cast_cols(dst, n, src2d, c0, w, dcol, key, kc=KC):
        kper = 16
        for k0 in range(0, kc, kper):
            k1 = min(kc, k0 + kper)
            DMA("cast", dst[n, :, k0:k1, dcol:dcol + w],
                src2d[k0 * 128:k1 * 128, c0:c0 + w].rearrange("(kc p) c -> p kc c", p=128),
                [], [(key, n, k0)])

    def wkeys(key, n, kc=KC):
        return [(key, n, k0) for k0 in range(0, kc, 16)]

    ev_splits = np.cumsum([0, 1024, 256, 256, 1024, 256, 512, 64, 8])
    o_qA, o_kA, o_vA, o_qB, o_cl, o_qi, o_ki, o_wi = [int(v) for v in ev_splits[:8]]
    for n in range(8):
        cast_cols(W_evfm, n, evwin_in, o_qA + n * 128, 128, 0, "W_evfm")
    for g in range(4):
        for hf in range(2):
            cast_cols(W_evfm, 8 + g, evwin_in, o_kA + g * 64, 64, hf * 64, "W_evfm")
    for n in range(8):
        cast_cols(W_evfm, 12 + n, evwin_in, o_qB + n * 128, 128, 0, "W_evfm")
    for n in range(4):
        cast_cols(W_evfm, 20 + n, evwin_in, o_qi + n * 128, 128, 0, "W_evfm")
    for hf in range(2):
        cast_cols(W_evfm, 24, evwin_in, o_ki, 64, hf * 64, "W_evfm")
    for g in range(4):
        for hf in range(2):
            cast_cols(W_evtm, g, evwin_in, o_vA + g * 64, 64, hf * 64, "W_evtm")
    for n in range(2):
        cast_cols(W_evtm, 4 + n, evwin_in, o_cl + n * 128, 128, 0, "W_evtm")
    cast_cols(W_evtm, 6, evwin_in, o_wi, 8, 0, "W_evtm")
    for n in range(16):
        cast_cols(W_evout, n, evwout_in, n * 128, 128, 0, "W_evout")
    for n in range(NFC):
        cast_cols(W_up[0], n, wup_in[0], n * 128, 128, 0, "W_up0")
    for n in range(16):
        cast_cols(W_dn[0], n, wdn_in[0], n * 128, 128, 0, "W_dn0", kc=44)
    for n in range(48):
        cast_cols(W_odin, n, odwin_in, n * 128, 128, 0, "W_odin")
    for n in range(16):
        cast_cols(W_odout, n, odwout_in, n * 128, 128, 0, "W_odout")
    for n in range(NFC):
        cast_cols(W_up[1], n, wup_in[1], n * 128, 128, 0, "W_up1")
    for n in range(16):
        cast_cols(W_dn[1], n, wdn_in[1], n * 128, 128, 0, "W_dn1", kc=44)

    ident = S.sbuf("ident", [128, 128], F32)
    identb = S.sbuf("identb", [128, 128], BF16)
    Jb = S.sbuf("Jb", [128, 128], BF16)
    onesb = S.sbuf("onesb", [128, 128], BF16)
    negonesb = S.sbuf("negonesb", [128, 128], BF16)
    negUb = S.sbuf("negUb", [128, 128], BF16)
    ctmp = S.sbuf("ctmp", [128, 128], F32)
    modv = S.sbuf("modv", [128, 192], F32)
    a1 = S.sbuf("a1", [128, 32], F32)
    a2 = S.sbuf("a2", [128, 32], F32)
    gmix = S.sbuf("gmix", [128, 32], F32)
    gffn = S.sbuf("gffn", [128, 32], F32)
    fing = S.sbuf("fing", [128, KC], F32)
    zero16 = S.sbuf("zero16", [128, KC], F32)
    convw = S.sbuf("convw", [128, 2 * 3 * NFC], F32)
    convb = S.sbuf("convb", [128, 2 * NFC], F32)
    epsb = S.sbuf("epsb", [128, 1], F32)
    DMA("sp", ident[:], cst["ident"], [], [ident])
    CP("dve", identb[:], ident[:], [ident], [identb])
    DMA("sp", ctmp[:], cst["Jmat"], [], [ctmp])
    CP("dve", Jb[:], ctmp[:], [ctmp], [Jb])
    DMA("sp", ctmp[:], cst["negU"], [Jb], [ctmp])
    CP("dve", negUb[:], ctmp[:], [ctmp], [negUb])
    MEMSET("dve", onesb[:], 1.0, [onesb])
    MEMSET("dve", negonesb[:], -1.0, [negonesb])
    MEMSET("dve", zero16[:], 0.0, [zero16])
    MEMSET("dve", epsb[:], EPS, [epsb])
    DMA("sp", gmix[:], gmix_in, [], [gmix])
    DMA("sp", gffn[:], gffn_in, [], [gffn])
    DMA("sp", fing[:], fing_in, [], [fing])
    DMA("sp", convw[:], convw_in, [], [convw])
    DMA("sp", convb[:], convb_in, [], [convb])

    class Scope:
        def __enter__(self):
            self.saved = S.stack
            S.stack = ExitStack()
            return self

        def __exit__(self, *a):
            S.barrier()
            S.stack.close()
            S.stack = self.saved
            return False

    with Scope():
        cact = S.sbuf("cact", [128, KC], F32)
        adab = S.sbuf("adab", [128, 192], F32)
        wblk = [S.sbuf(f"wblk{i}", [128, KC, 512], F32) for i in range(2)]
        psm = S.psum("psm", [128, 512], F32)
        DMA("sp", cact[:], cT_in, [], [cact])
        DMA("sp", adab[:], adab_in, [], [adab])
        ACT(cact[:], cact[:], AF.Silu, [cact], [cact])
        bi = 0
        for l in range(2):
            for cb in range(24):
                wb = wblk[bi % 2]
                bi += 1
                DMA("sp", wb[:], adaw_in[l][:, cb * 512:(cb + 1) * 512].rearrange("(kc p) n -> p kc n", p=128), [], [wb])
                for jj in range(4):
                    j = cb * 4 + jj
                    for kc in range(KC):
                        MM(psm[:, l * 96 + j:l * 96 + j + 1], wb[:, kc, jj * 128:(jj + 1) * 128], cact[:, kc:kc + 1],
                           kc == 0, kc == KC - 1, [wb, cact], [psm])
        TTo("dve", modv[:], psm[:, 0:192], adab[:], ALU.add, [psm, adab], [modv])
        for l in range(2):
            STT("dve", a1[:, l * 16:(l + 1) * 16], modv[:, l * 96 + 16:l * 96 + 32], 1.0, gmix[:, l * 16:(l + 1) * 16],
                ALU.add, ALU.mult, [modv, gmix], [a1])
            STT("dve", a2[:, l * 16:(l + 1) * 16], modv[:, l * 96 + 64:l * 96 + 80], 1.0, gffn[:, l * 16:(l + 1) * 16],
                ALU.add, ALU.mult, [modv, gffn], [a2])

    def mv(l, grp, fc):
        c = l * 96 + grp * 16 + fc
        return modv[:, c:c + 1]

    def norm_tile(xt, hT, sq, tmpf, pss, rstd, gain, shift, outf=None):
        ACT(sq[:, 0:KC, :], xt[:], AF.Square, [xt], [sq])
        for kc in range(KC):
            MM(pss[:], onesb[:], sq[:, kc, :], kc == 0, kc == KC - 1, [sq, onesb], [pss])
        ACT(rstd[:], pss[:], AF.Sqrt, [pss], [rstd], bias=epsb[:, 0:1], scale=1.0 / D)
        RECIP(rstd[:], rstd[:], [rstd], [rstd])
        for fc in range(KC):
            if shift is None:
                STT("dve", outf[:, fc, :], xt[:, fc, :], gain(fc), rstd[:], ALU.mult, ALU.mult, [xt, rstd], [(outf.name, fc)])
                continue
            tf = tmpf[fc % 2]
            STT("dve", tf[:], xt[:, fc, :], gain(fc), rstd[:], ALU.mult, ALU.mult, [xt, rstd], [tf])
            ACT(hT[:, fc, :], tf[:], AF.Identity, [tf], [(hT.name, fc)], bias=shift(fc), scale=1.0)

    def hkeys(hT):
        return [(hT.name, fc) for fc in range(KC)]

    class WRing:
        def __init__(self, name, n, kc):
            self.bufs = [S.sbuf(f"{name}{i}", [128, kc, 128], BF16) for i in range(n)]
            self.i = 0
            self.kc = kc

        def load(self, Wt, n, key):
            b = self.bufs[self.i % len(self.bufs)]
            self.i += 1
            DMA("sp", b[:], Wt[n], wkeys(key, n, self.kc), [b])
            return b

    xT_v = xT.rearrange("(fc p) t -> p fc t", p=128)
    oT_v = oT.rearrange("(fc p) t -> p fc t", p=128)

    if upto >= 1:
      with Scope():
        xtm = [S.sbuf(f"xtm{i}", [128, D], F32) for i in range(4)]
        xt = S.sbuf("xt", [128, KC, TT], F32)
        sq = S.sbuf("sq", [128, KC, TT], BF16)
        hT = S.sbuf("hT", [128, KC, TT], BF16)
        tmpf = [S.sbuf(f"tmpf{i}", [128, TT], F32) for i in range(2)]
        rstd = S.sbuf("rstd", [128, TT], F32)
        wtm = S.sbuf("wtm", [128, KC, 7 * 128], BF16)
        kvg = S.sbuf("kvg", [128, 256], F32)
        stg = [S.sbuf(f"stg{i}", [128, TT], BF16) for i in range(3)]
        junk = S.sbuf("junk", [128, 256], F32)
        ss = S.sbuf("ss", [128, 1], F32)
        cstg = S.sbuf("cstg", [128, 256], BF16)
        ctstg = S.sbuf("ctstg", [128, 2, 128], BF16)
        wistg = S.sbuf("wistg", [128, 8], F32)
        wr = WRing("wr", 6, KC)
        pst = [S.psum(f"pst{i}", [128, TT], F32) for i in range(2)]
        pss = S.psum("pss", [128, TT], F32)
        psg = [S.psum(f"psg{i}", [128, TT], F32) for i in range(2)]
        psA = S.psum("psA", [128, TT], F32)
        psB = S.psum("psB", [128, TT], F32)
        pstb = S.psum("pstb", [128, 128], BF16)
        for n in range(7):
            DMA("sp", wtm[:, :, n * 128:(n + 1) * 128], W_evtm[n], wkeys("W_evtm", n), [wtm])
        DMA("sp", kvg[:], kvg_in, [], [kvg])
        fm_dst = [(qAT, n) for n in range(8)] + [(kA2T, n) for n in range(4)] + [(qBT, n) for n in range(8)] + \
                 [(qiT, n) for n in range(4)] + [(kiT2, 0)]
        si = 0
        for tt in range(NT):
            tsl = slice(tt * TT, (tt + 1) * TT)
            for tb in range(4):
                blk = tt * 4 + tb
                DMA("sp", xtm[tb][:], x_in[blk * 128:(blk + 1) * 128, :], [], [xtm[tb]])
            for fc in range(KC):
                p = pst[fc % 2]
                for tb in range(4):
                    TR(p[:, tb * 128:(tb + 1) * 128], xtm[tb][:, fc * 128:(fc + 1) * 128], ident[:], [xtm[tb], ident], [p])
                EVAC(xt[:, fc, :], p[:], [p], [xt])
            DMA("pool", xT_v[:, :, tsl], xt[:], [xt], [])
            norm_tile(xt, hT, sq, tmpf, pss, rstd, lambda fc: a1[:, fc:fc + 1], lambda fc: mv(0, 0, fc))
            for n in range(25):
                wb = wr.load(W_evfm, n, "W_evfm")
                p = psg[n % 2]
                for kc in range(KC):
                    MM(p[:], wb[:, kc, :], hT[:, kc, :], kc == 0, kc == KC - 1, [wb] + hkeys(hT), [p])
                sb = stg[si % 3]
                si += 1
                EVAC(sb[:], p[:], [p], [sb])
                dst, dn = fm_dst[n]
                DMA("pool", dst[dn * 128:(dn + 1) * 128, tsl], sb[:], [sb], [])
            for tb in range(4):
                blk = tt * 4 + tb
                bsl = slice(blk * 128, (blk + 1) * 128)
                for kc in range(KC):
                    MM(psA[:], hT[:, kc, tb * 128:(tb + 1) * 128], wtm[:, kc, 0:512], kc == 0, kc == KC - 1, [wtm] + hkeys(hT), [psA])
                for kc in range(KC):
                    MM(psB[:, 0:264], hT[:, kc, tb * 128:(tb + 1) * 128], wtm[:, kc, 512:776], kc == 0, kc == KC - 1, [wtm] + hkeys(hT), [psB])
                sb = stg[si % 3]
                si += 1
                EVAC(sb[:], psA[:], [psA], [sb])
                DMA("pool", vA2[bsl, :], sb[:], [sb], [])
                ACT(junk[:], psB[:, 0:256], AF.Square, [psB], [junk, ss], accum=ss[:])
                ACT(ss[:], ss[:], AF.Sqrt, [ss], [ss], bias=epsb[:, 0:1], scale=1.0 / 256)
                RECIP(ss[:], ss[:], [ss], [ss])
                STT("dve", cstg[:], psB[:, 0:256], ss[:, 0:1], kvg[:], ALU.mult, ALU.mult, [psB, ss, kvg], [cstg])
                TS("dve", wistg[:], psB[:, 256:264], float(8 ** -0.5 * 64 ** -0.5), None, ALU.mult, None, [psB], [wistg])
                DMA("pool", ckv[bsl, :], cstg[:], [cstg], [])
                DMA("pool", wi_d[bsl, :], wistg[:], [wistg], [])
                for rc in range(2):
                    TR(pstb[:], cstg[:, rc * 128:(rc + 1) * 128], identb[:], [cstg, identb], [pstb])
                    EVAC(ctstg[:, rc, :], pstb[:], [pstb], [ctstg])
                DMA("pool", ckvT.rearrange("(rc p) t -> p rc t", p=128)[:, :, bsl], ctstg[:], [ctstg], [])
    if upto >= 2:
      with Scope():
        relb = S.sbuf("relb", [32, 32], F32)
        ohr = S.sbuf("ohr", [32, 384], F32)
        gsb = S.sbuf("gsb", [32, 384], F32)
        psb_ = S.psum("psb_", [128, TT], F32)
        DMA("sp", relb[:], relb_in, [], [relb])
        DMA("sp", ohr[:], cst["onehotR"], [], [ohr])
        MM(psb_[0:32, 0:384], relb[:], ohr[:], True, True, [relb, ohr], [psb_])
        CP("dve", gsb[:], psb_[0:32, 0:384], [psb_], [gsb])
        DMA("pool", gvec, gsb[:], [gsb], [])

    def toeplitz_src(h):
        return bass.AP(tensor=gvec.tensor, offset=gvec.offset + h * 384, ap=[[1, 128], [128, 2], [1, 128]])

    if upto >= 2:
      with Scope():
        biasA = S.sbuf("biasA", [128, 16, 2, 128], BF16)
        btmp = [S.sbuf(f"btmp{i}", [128, 2, 128], F32) for i in range(2)]
        swm = S.sbuf("swm", [128, 2, 128], F32)
        esink = S.sbuf("esink", [128, 16], F32)
        qa = S.sbuf("qa", [128, 8, TT], BF16)
        ka = S.sbuf("ka", [128, 4, 640], BF16)
        va = S.sbuf("va", [128, 5, 512], BF16)
        Pb = [S.sbuf(f"Pb{i}", [128, 512], BF16) for i in range(2)]
        den = [S.sbuf(f"den{i}", [128, 128], F32) for i in range(2)]
        ostg = S.sbuf("ostg", [128, 8, TT], BF16)
        psS = [S.psum(f"psS{i}", [128, 512], F32) for i in range(2)]
        psO = [S.psum(f"psO{i}", [128, 512], F32) for i in range(2)]
        DMA("sp", swm[:], cst["swamask"].rearrange("p (a t) -> p a t", a=2), [], [swm])
        DMA("sp", esink[:], sinks_in, [], [esink])
        ACT(esink[:], esink[:], AF.Exp, [esink], [esink])
        for h in range(16):
            bt = btmp[h % 2]
            DMA("sp", bt[:], toeplitz_src(h), [], [bt])
            STT("dve", biasA[:, h, :, :], bt[:], 8.0, swm[:], ALU.mult, ALU.add, [bt, swm], [biasA])
        it = 0
        for qt in range(NT):
            tsl = slice(qt * TT, (qt + 1) * TT)
            DMA("sp", qa[:], qAT.rearrange("(c p) t -> p c t", p=128)[:, :, tsl], [], [qa])
            if qt == 0:
                DMA("sp", ka[:, :, 128:640], kA2T.rearrange("(g p) t -> p g t", p=128)[:, :, 0:512], [], [ka])
                DMA("sp", va[:, 1:5, :], vA2[0:512, :].rearrange("(b s) c -> s b c", s=128), [], [va])
            else:
                DMA("sp", ka[:], kA2T.rearrange("(g p) t -> p g t", p=128)[:, :, qt * TT - 128:(qt + 1) * TT], [], [ka])
                DMA("sp", va[:], vA2[qt * TT - 128:(qt + 1) * TT, :].rearrange("(b s) c -> s b c", s=128), [], [va])
            for j in range(4):
                qb = qt * 4 + j
                dis = (0,) if qb == 0 else (0, 1)
                for c in range(8):
                    g = c // 2
                    Sb = psS[it % 2]
                    Ob = psO[it % 2]
                    P = Pb[it % 2]
                    it += 1
                    for e in range(2):
                        h = 2 * c + e
                        hs = slice(e * 64, (e + 1) * 64)
                        for di in dis:
                            kbi = 1 + j - di
                            slot = e * 2 + di
                            MM(Sb[:, slot * 128:(slot + 1) * 128], ka[hs, g, kbi * 128:(kbi + 1) * 128], qa[hs, c, j * 128:(j + 1) * 128],
                               True, False, [ka, qa], [Sb])
                            MM(Sb[:, slot * 128:(slot + 1) * 128], Jb[:], biasA[:, h, di, :], False, True, [Jb, biasA], [Sb])
                    if qb == 0:
                        for e in range(2):
                            ACT(P[:, e * 256:e * 256 + 128], Sb[:, e * 256:e * 256 + 128], AF.Exp, [Sb], [P], scale=0.125)
                    else:
                        ACT(P[:], Sb[:], AF.Exp, [Sb], [P], scale=0.125)
                    for e in range(2):
                        for di in dis:
                            kbi = 1 + j - di
                            slot = e * 2 + di
                            MM(Ob[:, e * 128:(e + 1) * 128], va[:, kbi, g * 128:(g + 1) * 128], P[:, slot * 128:(slot + 1) * 128],
                               di == dis[0], di == dis[-1], [va, P], [Ob])
                    for e in range(2):
                        for di in dis:
                            slot = e * 2 + di
                            MM(Ob[:, (2 + e) * 128:(3 + e) * 128], onesb[:], P[:, slot * 128:(slot + 1) * 128],
                               di == dis[0], di == dis[-1], [onesb, P], [Ob])
                    for e in range(2):
                        h = 2 * c + e
                        hs = slice(e * 64, (e + 1) * 64)
                        dn = den[e]
                        TS("dve", dn[hs, :], Ob[hs, (2 + e) * 128:(3 + e) * 128], esink[hs, h:h + 1], None, ALU.add, None, [Ob, esink], [dn])
                        RECIP(dn[hs, :], dn[hs, :], [dn], [dn])
                        TTo("dve", ostg[hs, c, j * 128:(j + 1) * 128], Ob[hs, e * 128:(e + 1) * 128], dn[hs, :], ALU.mult, [Ob, dn], [ostg])
            DMA("pool", oT_v[:, 0:8, tsl], ostg[:], [ostg], [])

    if upto >= 3:
      with Scope():
        NBIS = 20
        ki2 = S.sbuf("ki2", [128, T], BF16)
        ckvT_sb = S.sbuf("ckvT_sb", [128, 2, T], BF16)
        ckv_sb = S.sbuf("ckv_sb", [128, NB, 256], BF16)
        wuk_sb = S.sbuf("wuk_sb", [128, 8, 256], BF16)
        wuv2 = S.sbuf("wuv2", [128, 16, 2, 128], BF16)
        biasB = S.sbuf("biasB", [128, 16, 2, 128], BF16)
        farb = S.sbuf("farb", [128, 16], F32)
        idg = S.sbuf("idg", [128, 128], F32)
        p2a = S.sbuf("p2a", [128, NBIS], F32)
        p2b = S.sbuf("p2b", [128, NBIS], F32)
        with Scope():
            wukf = S.sbuf("wukf", [128, 8, 256], F32)
            wuvf = S.sbuf("wuvf", [128, 16, 2, 64], F32)
            btmp = [S.sbuf(f"btmpd{i}", [128, 2, 128], F32) for i in range(2)]
            DMA("sp", wukf[:], wuk_in.rearrange("(c e) d r -> (e d) c r", e=2), [], [wukf])
            CP("dve", wuk_sb[:], wukf[:], [wukf], [wuk_sb])
            DMA("sp", wuvf[:], wuv_in.rearrange("h (rc r) d -> r h rc d", r=128), [], [wuvf])
            CP("dve", wuv2[:, :, :, 0:64], wuvf[:], [wuvf], [wuv2])
            CP("dve", wuv2[:, :, :, 64:128], wuvf[:], [wuvf], [wuv2])
            DMA("sp", farb[:], bass.AP(tensor=gvec.tensor, offset=gvec.offset + 16 * 384 + 382, ap=[[0, 128], [384, 16]]), [], [farb],
                allow_slow_non_contiguous=True)
            for h in range(16):
                bt = btmp[h % 2]
                DMA("sp", bt[:], toeplitz_src(16 + h), [], [bt])
                TS("dve", biasB[:, h, :, :], bt[:], farb[:, h:h + 1], None, ALU.subtract, None, [bt, farb], [biasB])
        score = [S.sbuf(f"score{i}", [128, T], F32) for i in range(2)]
        junkb = S.sbuf("junkb", [128, T], NEGM_DT)
        negm = [[S.sbuf(f"negm{p}_{i}", [128, T], NEGM_DT) for i in range(4)] for p in range(2)]
        rbuf = [S.sbuf(f"rbuf{i}", [128, 512], F32) for i in range(2)]
        bis = [S.sbuf(f"bis{i}", [128, 8], F32) for i in range(2)]
        htab = [S.sbuf(f"htab{i}", [128, 2, NBIS], F32) for i in range(2)]
        qi = S.sbuf("qi", [128, 4, TT], BF16)
        wi4 = S.sbuf("wi4", [128, 4, 8], F32)
        qbt = S.sbuf("qbt", [128, 8, TT], BF16)
        ql = [S.sbuf(f"ql{i}", [128, 2, TT], BF16) for i in range(2)]
        Pb = [S.sbuf(f"Pd{i}", [128, 512], BF16) for i in range(3)]
        rec = S.sbuf("rec", [128, 512], F32)
        olat = S.sbuf("olat", [128, 2, 512], BF16)
        ostg = [S.sbuf(f"ostgd{i}", [128, TT], BF16) for i in range(2)]
        psi = [S.psum(f"psi{i}", [128, 512], F32) for i in range(2)]
        psq = S.psum("psq", [128, 512], F32)
        pss2 = [S.psum(f"pss2{i}", [128, 512], F32) for i in range(2)]
        Oacc = [S.psum(f"Oacc{i}", [128, 512], F32) for i in range(2)]
        Dacc = S.psum("Dacc", [128, 512], F32)
        DMA("sp", ki2[:], kiT2, [], [ki2])
        DMA("sp", ckvT_sb[:], ckvT.rearrange("(rc p) t -> p rc t", p=128), [], [ckvT_sb])
        DMA("sp", ckv_sb[:], ckv.rearrange("(b s) r -> s b r", s=128), [], [ckv_sb])
        DMA("sp", idg[:], cst["idxdiag"], [], [idg])
        DMA("sp", p2a[:], cst["pow2a"], [], [p2a])
        DMA("sp", p2b[:], cst["pow2b"], [], [p2b])
        ri = [0]
        sci = [0]
        bis_tasks = []

        def index_qblock(qt, j):
            tsl = slice(qt * TT, (qt + 1) * TT)
            nkb = 4 * qt + 4
            if j == 0:
                DMA("sp", qi[:], qiT.rearrange("(c p) t -> p c t", p=128)[:, :, tsl], [], [qi])
                DMA("sp", wi4[:], wi_d[tsl, :].rearrange("(j t) h -> t j h", t=128), [], [wi4])
            if True:
                qb = qt * 4 + j
                nk = (qb + 1) * 128
                sc = score[sci[0] % 2]
                bs = bis[sci[0] % 2]
                ht = htab[sci[0] % 2]
                sci[0] += 1
                nm = negm[qt % 2][j]
                for h in range(8):
                    c, e = h // 2, h % 2
                    hs = slice(e * 64, (e + 1) * 64)
                    for k0 in range(0, nk, 512):
                        w = min(512, nk - k0)
                        p = psi[ri[0] % 2]
                        rb = rbuf[ri[0] % 2]
                        ri[0] += 1
                        MM(p[:, 0:w], qi[hs, c, j * 128:(j + 1) * 128], ki2[hs, k0:k0 + w], True, True, [qi, ki2], [p])
                        ACT(rb[:, 0:w], p[:, 0:w], AF.Relu, [p], [rb])
                        sk = (sc.name, k0)
                        if h == 0:
                            TS("dve", sc[:, k0:k0 + w], rb[:, 0:w], wi4[:, j, 0:1], None, ALU.mult, None, [rb, wi4], [sk, sc])
                        else:
                            STT("dve", sc[:, k0:k0 + w], rb[:, 0:w], wi4[:, j, h:h + 1], sc[:, k0:k0 + w], ALU.mult, ALU.add,
                                [rb, wi4, sk], [sk])
                allk = [(sc.name, k0) for k0 in range(0, nk, 512)]
                S.op("dve", lambda e_, sc=sc, bs=bs, nk=nk: e_.tensor_reduce(bs[:, 0:1], sc[:, 0:nk], AX.X, ALU.max, apply_absolute_value=True),
                     allk, [bs])
                TS("dve", bs[:, 0:1], bs[:, 0:1], 1.0, None, ALU.add, None, [bs], [bs])
                TS("dve", ht[:, 0, :], p2a[:], bs[:, 0:1], None, ALU.mult, None, [bs, p2a], [ht])
                TS("dve", ht[:, 1, :], p2b[:], bs[:, 0:1], None, ALU.mult, None, [bs, p2b], [ht])
                MEMSET("dve", bs[:, 1:2], 0.0, [bs])
                TTo("dve", sc[:, qb * 128:(qb + 1) * 128], sc[:, qb * 128:(qb + 1) * 128], idg[:], ALU.add, allk + [idg, bs], [sc])

                def bis_iter(k, sc=sc, bs=bs, ht=ht, nk=nk):
                    S.op("dve", lambda e_: e_.tensor_scalar(junkb[:, 0:nk], sc[:, 0:nk], bs[:, 1:2], 0.0, ALU.is_ge, ALU.add,
                                                            accum_out=bs[:, 2:3]), [sc, bs], [junkb, bs])
                    TS("dve", bs[:, 4:5], bs[:, 1:2], ht[:, 1, k:k + 1], None, ALU.subtract, None, [bs, ht], [bs])
                    STT("dve", bs[:, 1:2], bs[:, 2:3], 255.5, ht[:, 0, k:k + 1], ALU.is_ge, ALU.mult, [bs, ht], [bs])
                    TTo("dve", bs[:, 1:2], bs[:, 1:2], bs[:, 4:5], ALU.add, [bs], [bs])

                def bis_final(sc=sc, bs=bs, ht=ht, nk=nk, nm=nm, nkb=nkb):
                    TS("dve", bs[:, 5:6], bs[:, 1:2], ht[:, 1, NBIS - 1:NBIS], None, ALU.subtract, None, [bs, ht], [bs])
                    S.op("pool", lambda e_: e_.tensor_scalar(nm[:, 0:nk], sc[:, 0:nk], bs[:, 5:6], NEGM_V, ALU.is_lt, ALU.mult), [sc, bs], [nm])
                    if nk < nkb * 128:
                        MEMSET("pool", nm[:, nk:nkb * 128], NEGM_V, [nm])

                for k in range(NBIS):
                    bis_tasks.append((qb, lambda k=k, f=bis_iter: f(k)))
                bis_tasks.append((qb, bis_final))

        def run_tasks(n=None, older_than=None):
            while bis_tasks:
                if older_than is not None and bis_tasks[0][0] >= older_than:
                    break
                if n is not None:
                    if n <= 0:
                        break
                    n -= 1
                bis_tasks.pop(0)[1]()

        def stage_attn(qt):
            tsl = slice(qt * TT, (qt + 1) * TT)
            nkb = 4 * qt + 4
            nmt = negm[qt % 2]
            DMA("sp", qbt[:], qBT.rearrange("(c p) t -> p c t", p=128)[:, :, tsl], [], [qbt])
            ditems = [(h, kb) for h in range(16) for kb in range(nkb)]

            def dstageA(i):
                h, kb = ditems[i]
                c, e = h // 2, h % 2
                hs = slice(e * 64, (e + 1) * 64)
                q_ = ql[h % 2]
                if kb == 0:
                    for rc in range(2):
                        MM(psq[:], wuk_sb[hs, c, rc * 128:(rc + 1) * 128], qbt[hs, c, :], True, True, [wuk_sb, qbt], [psq])
                        EVAC(q_[:, rc, :], psq[:], [psq], [q_], scale=0.125)
                ksl = slice(kb * 128, (kb + 1) * 128)
                Sb = pss2[i % 2]
                P = Pb[i % 3]
                ops = [(Sb[:], ckvT_sb[:, 0, ksl], q_[:, 0, :], [ckvT_sb, q_]),
                       (Sb[:], ckvT_sb[:, 1, ksl], q_[:, 1, :], [ckvT_sb, q_])]
                for j in range(4):
                    qb = qt * 4 + j
                    cs = slice(j * 128, (j + 1) * 128)
                    ops.append((Sb[:, cs], nmt[j][:, ksl], identb[:], [nmt[j], identb]))
                    if kb == qb:
                        ops.append((Sb[:, cs], Jb[:], biasB[:, h, 0, :], [Jb, biasB]))
                    if kb == qb - 1:
                        ops.append((Sb[:, cs], Jb[:], biasB[:, h, 1, :], [Jb, biasB]))
                for oi, (o_, l_, r_, rd) in enumerate(ops):
                    MM(o_, l_, r_, oi == 0, oi == len(ops) - 1, rd, [Sb])
                ACT(P[:], Sb[:], AF.Exp, [Sb, farb], [P], bias=farb[:, h:h + 1], scale=1.0)

            def dstageB(i):
                h, kb = ditems[i]
                c, e = h // 2, h % 2
                hs = slice(e * 64, (e + 1) * 64)
                P = Pb[i % 3]
                MM(Oacc[0][:], ckv_sb[:, kb, 0:128], P[:], kb == 0, kb == nkb - 1, [ckv_sb, P], [Oacc[0]])
                MM(Oacc[1][:], ckv_sb[:, kb, 128:256], P[:], kb == 0, kb == nkb - 1, [ckv_sb, P], [Oacc[1]])
                MM(Dacc[:], onesb[:], P[:], kb == 0, kb == nkb - 1, [onesb, P], [Dacc])
                if kb == nkb - 1:
                    og = ostg[c % 2]
                    RECIP(rec[:], Dacc[:], [Dacc], [rec])
                    for rc in range(2):
                        TTo("dve", olat[:, rc, :], Oacc[rc][:], rec[:], ALU.mult, [Oacc[rc], rec], [olat])
                    MM(psq[:], wuv2[:, h, 0, :], olat[:, 0, :], True, False, [wuv2, olat], [psq])
                    MM(psq[:], wuv2[:, h, 1, :], olat[:, 1, :], False, True, [wuv2, olat], [psq])
                    EVAC(og[hs, :], psq[hs, :], [psq], [og])
                    if e == 1:
                        DMA("pool", oT[(8 + c) * 128:(9 + c) * 128, tsl], og[:], [og], [])

            nd = len(ditems)
            rate = 2 if nd < 160 else 1
            marks = {(nd * (jn + 1)) // 5: jn for jn in range(4)}
            for i in range(nd + 1):
                if i < nd:
                    dstageA(i)
                if i >= 1:
                    dstageB(i - 1)
                run_tasks(n=rate)
                if i in marks and qt + 1 < NT:
                    run_tasks(older_than=(qt + 1) * 4 + marks[i] - 1)
                    index_qblock(qt + 1, marks[i])
            run_tasks()

        for j in range(4):
            index_qblock(0, j)
            run_tasks()
        for qt in range(NT):
            stage_attn(qt)

    def out_ffn_phase(l, Wout, wkey):
        with Scope():
            xt = S.sbuf("xt", [128, KC, TT], F32)
            ot = S.sbuf("ot", [128, KC, TT], BF16)
            ubig = S.sbuf("ubig", [128, 44, TT], BF16)
            tmpf = [S.sbuf(f"tmpf{i}", [128, TT], F32) for i in range(2)]
            rstd = S.sbuf("rstd", [128, TT], F32)
            ub = [S.sbuf(f"ub{i}", [128, TT + 2], F32) for i in range(2)]
            acc = [S.sbuf(f"acc{i}", [128, TT], F32) for i in range(2)]
            carry = S.sbuf("carry", [128, NFC, 2], F32)
            wr = WRing("wr", 6, KC)
            wrd = WRing("wrd", 2, 44)
            psg = [S.psum(f"psg{i}", [128, TT], F32) for i in range(2)]
            pss = S.psum("pss", [128, TT], F32)
            psu = [S.psum(f"psu{i}", [128, TT], F32) for i in range(2)]
            psd = [S.psum(f"psd{i}", [128, TT], F32) for i in range(2)]
            cwo = l * 3 * NFC
            for tt in range(NT):
                tsl = slice(tt * TT, (tt + 1) * TT)
                DMA("sp", xt[:], xT_v[:, :, tsl], [], [xt] + hkeys(xt))
                DMA("sp", ot[:], oT_v[:, :, tsl], [], [ot] + hkeys(ot))
                for n in range(KC):
                    wb = wr.load(Wout, n, wkey)
                    p = psg[n % 2]
                    for kc in range(KC):
                        MM(p[:], wb[:, kc, :], ot[:, kc, :], kc == 0, kc == KC - 1, [wb, ot], [p])
                    STT("dve", xt[:, n, :], p[:], mv(l, 2, n), xt[:, n, :], ALU.mult, ALU.add, [p, (xt.name, n)], [(xt.name, n)])
                S.op("pool", lambda e_: e_.engine_nop(), hkeys(xt), [xt])
                norm_tile(xt, ot, ubig, tmpf, pss, rstd, lambda fc: a2[:, l * 16 + fc:l * 16 + fc + 1], lambda fc: mv(l, 3, fc))
                S.op("pool", lambda e_: e_.engine_nop(), hkeys(ot), [ot])
                for j in range(44):
                    for which, cidx in ((0, j), (1, 44 + j)):
                        wb = wr.load(W_up[l], cidx, f"W_up{l}")
                        p = psu[which]
                        u = ub[which]
                        a = acc[which]
                        for kc in range(KC):
                            MM(p[:], wb[:, kc, :], ot[:, kc, :], kc == 0, kc == KC - 1, [wb, ot], [p])
                        CP("act", u[:, 2:TT + 2], p[:], [p], [u])
                        if tt == 0:
                            MEMSET("pool", u[:, 0:2], 0.0, [u])
                        else:
                            CP("pool", u[:, 0:2], carry[:, cidx, :], [(carry.name, cidx)], [u])
                        ACT(a[:], p[:], AF.Identity, [p], [a], bias=convb[:, l * NFC + cidx:l * NFC + cidx + 1],
                            scale=convw[:, cwo + 2 * NFC + cidx:cwo + 2 * NFC + cidx + 1])
                        STT("dve", a[:], u[:, 1:TT + 1], convw[:, cwo + NFC + cidx:cwo + NFC + cidx + 1], a[:], ALU.mult, ALU.add, [u, a], [a])
                        STT("dve", a[:], u[:, 0:TT], convw[:, cwo + cidx:cwo + cidx + 1], a[:], ALU.mult, ALU.add, [u, a], [a])
                        CP("pool", carry[:, cidx, :], u[:, TT:TT + 2], [u], [(carry.name, cidx)])
                    ACT(acc[0][:], acc[0][:], AF.Silu, [acc[0]], [acc[0]])
                    TTo("dve", ubig[:, j, :], acc[0][:], acc[1][:], ALU.mult, [acc[0], acc[1]], [(ubig.name, j)])
                S.op("pool", lambda e_: e_.engine_nop(), [(ubig.name, j) for j in range(44)], [ubig])
                for n in range(KC):
                    wd = wrd.load(W_dn[l], n, f"W_dn{l}")
                    p = psd[n % 2]
                    for kc in range(44):
                        MM(p[:], wd[:, kc, :], ubig[:, kc, :], kc == 0, kc == 43, [wd, ubig], [p])
                    STT("dve", xt[:, n, :], p[:], mv(l, 5, n), xt[:, n, :], ALU.mult, ALU.add, [p, (xt.name, n)], [(xt.name, n)])
                S.op("pool", lambda e_: e_.engine_nop(), hkeys(xt), [xt])
                DMA("pool", xT_v[:, :, tsl], xt[:], [xt], [])

    if upto >= 4:
        out_ffn_phase(0, W_evout, "W_evout")

    if upto >= 5:
      with Scope():
        xt = S.sbuf("xt", [128, KC, TT], F32)
        sq = S.sbuf("sq", [128, KC, TT], BF16)
        hT = S.sbuf("hT", [128, KC, TT], BF16)
        tmpf = [S.sbuf(f"tmpf{i}", [128, TT], F32) for i in range(2)]
        rstd = S.sbuf("rstd", [128, TT], F32)
        stg = [S.sbuf(f"stg{i}", [128, TT], BF16) for i in range(3)]
        wv = [S.sbuf(f"wv{i}", [128, KC, 512], BF16) for i in range(2)]
        wr = WRing("wr", 6, KC)
        pss = S.psum("pss", [128, TT], F32)
        psg = [S.psum(f"psg{i}", [128, TT], F32) for i in range(2)]
        psv = [S.psum(f"psv{i}", [128, TT], F32) for i in range(2)]
        si = 0
        for tt in range(NT):
            tsl = slice(tt * TT, (tt + 1) * TT)
            DMA("sp", xt[:], xT_v[:, :, tsl], [], [xt])
            norm_tile(xt, hT, sq, tmpf, pss, rstd, lambda fc: a1[:, 16 + fc:16 + fc + 1], lambda fc: mv(1, 0, fc))
            for n in range(32):
                wb = wr.load(W_odin, n, "W_odin")
                p = psg[n % 2]
                for kc in range(KC):
                    MM(p[:], wb[:, kc, :], hT[:, kc, :], kc == 0, kc == KC - 1, [wb] + hkeys(hT), [p])
                sb = stg[si % 3]
                si += 1
                if n < 16:
                    EVAC(sb[:], p[:], [p], [sb], scale=float(128 ** -0.5))
                    DMA("pool", q1T[n * 128:(n + 1) * 128, tsl], sb[:], [sb], [])
                else:
                    EVAC(sb[:], p[:], [p], [sb])
                    DMA("pool", k1T[(n - 16) * 128:(n - 15) * 128, tsl], sb[:], [sb], [])
            for cg in range(4):
                w_ = wv[cg % 2]
                for i in range(4):
                    n = 32 + cg * 4 + i
                    DMA("sp", w_[:, :, i * 128:(i + 1) * 128], W_odin[n], wkeys("W_odin", n), [w_])
                for tb in range(4):
                    blk = tt * 4 + tb
                    p = psv[tb % 2]
                    for kc in range(KC):
                        MM(p[:], hT[:, kc, tb * 128:(tb + 1) * 128], w_[:, kc, :], kc == 0, kc == KC - 1, [w_] + hkeys(hT), [p])
                    sb = stg[si % 3]
                    si += 1
                    EVAC(sb[:], p[:], [p], [sb])
                    DMA("pool", v1[blk * 128:(blk + 1) * 128, cg * 512:(cg + 1) * 512], sb[:], [sb], [])

    if upto >= 6:
      with Scope():
        sb01b = S.sbuf("sb01b", [128, 4, TT], BF16)
        sbnegb = S.sbuf("sbnegb", [128, 4, TT], BF16)
        kh = [S.sbuf(f"kh{i}", [128, T], BF16) for i in range(2)]
        vh = [S.sbuf(f"vh{i}", [128, NB, 128], BF16) for i in range(2)]
        qh = [S.sbuf(f"qh{i}", [128, TT], BF16) for i in range(3)]
        ebuf = [S.sbuf(f"ebuf{i}", [128, TT], F32) for i in range(2)]
        spb = [S.sbuf(f"spb{i}", [128, TT], BF16) for i in range(4)]
        ccs = [S.sbuf(f"ccs{i}", [128, TT], F32) for i in range(2)]
        tmpe = [S.sbuf(f"tmpe{i}", [128, TT], F32) for i in range(2)]
        Ab = [S.sbuf(f"Ab{i}", [128, TT], BF16) for i in range(2)]
        ostg = [S.sbuf(f"ostgs{i}", [128, TT], BF16) for i in range(2)]
        Za = [S.psum(f"Za{i}", [128, TT], F32) for i in range(3)]
        Eb = [S.psum(f"Eb{i}", [128, TT], F32) for i in range(2)]
        Cc = S.psum("Cc", [128, TT], F32)
        Oa = [S.psum(f"Oa{i}", [128, TT], F32) for i in range(2)]
        with Scope():
            cf = S.sbuf("cf", [128, 4, TT], F32)
            DMA("sp", cf[:], cst["sb01"].rearrange("p (r t) -> p r t", r=4), [], [cf])
            CP("dve", sb01b[:], cf[:], [cf], [sb01b])
            DMA("sp", cf[:], cst["sbneg"].rearrange("p (r t) -> p r t", r=4), [sb01b], [cf])
            CP("dve", sbnegb[:], cf[:], [cf], [sbnegb])
        items = []
        for h in range(16):
            for qt in range(NT):
                nkb = 4 * qt + 4
                for kb in range(nkb - 1, -1, -1):
                    items.append((h, qt, kb, kb == nkb - 1, kb == 0))
        AHEAD = 2
        state = {}

        def stageA(i):
            h, qt, kb, first, last = items[i]
            if first and qt == 0:
                k_ = kh[h % 2]
                v_ = vh[h % 2]
                DMA("sp", k_[:], k1T[h * 128:(h + 1) * 128, :], [], [k_])
                DMA("sp", v_[:], v1[:, h * 128:(h + 1) * 128].rearrange("(b s) d -> s b d", s=128), [], [v_])
            if first:
                q_ = qh[(h * NT + qt) % 3]
                DMA("sp", q_[:], q1T[h * 128:(h + 1) * 128, qt * TT:(qt + 1) * TT], [], [q_])
            k_ = kh[h % 2]
            q_ = qh[(h * NT + qt) % 3]
            relb_ = kb - 4 * qt
            ksl = slice(kb * 128, (kb + 1) * 128)
            z = Za[i % 3]
            eb = ebuf[i % 2]
            sp = spb[i % 4]
            MM(z[:], k_[:, ksl], q_[:], True, True, [k_, q_], [z])
            ACT(eb[:], z[:], AF.Exp, [z], [eb])
            ACT(sp[:], eb[:], AF.Ln, [eb], [sp], bias=1.0)
            if relb_ >= 0:
                TTo("pool", sp[:], sp[:], sb01b[:, relb_, :], ALU.mult, [sp, sb01b], [sp])

        def stageB(i):
            h, qt, kb, first, last = items[i]
            k_ = kh[h % 2]
            v_ = vh[h % 2]
            q_ = qh[(h * NT + qt) % 3]
            O_ = Oa[(h * NT + qt) % 2]
            og = ostg[(h * NT + qt) % 2]
            relb_ = kb - 4 * qt
            ksl = slice(kb * 128, (kb + 1) * 128)
            sp = spb[i % 4]
            E = Eb[i % 2]
            A = Ab[i % 2]
            cc = ccs[i % 2]
            te = tmpe[i % 2]
            MM(E[:], k_[:, ksl], q_[:], True, False, [k_, q_], [E])
            MM(E[:], negUb[:], sp[:], False, relb_ < 0, [negUb, sp], [E])
            if relb_ >= 0:
                MM(E[:], identb[:], sbnegb[:, relb_, :], False, True, [identb, sbnegb], [E])
            if first:
                ACT(A[:], E[:], AF.Exp, [E], [A])
            else:
                CP("dve", cc[:], Cc[:], [Cc], [cc])
                TTo("dve", te[:], E[:], cc[:], ALU.add, [E, cc], [te])
                ACT(A[:], te[:], AF.Exp, [te], [A])
            MM(Cc[:], negonesb[:], sp[:], first, last, [negonesb, sp], [Cc])
            MM(O_[:], v_[:, kb, :], A[:], first, last, [v_, A], [O_])
            if last:
                EVAC(og[:], O_[:], [O_], [og])
                DMA("pool", oT[h * 128:(h + 1) * 128, qt * TT:(qt + 1) * TT], og[:], [og], [])

        n_it = len(items)
        for i in range(n_it + AHEAD):
            if i < n_it:
                stageA(i)
            if i >= AHEAD:
                stageB(i - AHEAD)

    if upto >= 7:
        out_ffn_phase(1, W_odout, "W_odout")

    if upto >= 8:
      with Scope():
        xt = S.sbuf("xt", [128, KC, TT], F32)
        sq = S.sbuf("sq", [128, KC, TT], BF16)
        yf = S.sbuf("yf", [128, KC, TT], F32)
        rstd = S.sbuf("rstd", [128, TT], F32)
        otm = [S.sbuf(f"otm{i}", [128, D], F32) for i in range(2)]
        pss = S.psum("pss", [128, TT], F32)
        pso = [S.psum(f"pso{i}", [128, TT], F32) for i in range(2)]
        for tt in range(NT):
            tsl = slice(tt * TT, (tt + 1) * TT)
            DMA("sp", xt[:], xT_v[:, :, tsl], [], [xt])
            norm_tile(xt, None, sq, None, pss, rstd, lambda fc: fing[:, fc:fc + 1], None, outf=yf)
            for tb in range(4):
                blk = tt * 4 + tb
                om = otm[tb % 2]
                for fg in range(4):
                    p = pso[fg % 2]
                    for i in range(4):
                        fc = fg * 4 + i
                        TR(p[:, i * 128:(i + 1) * 128], yf[:, fc, tb * 128:(tb + 1) * 128], ident[:], [(yf.name, fc), ident], [p])
                    EVAC(om[:, fg * 512:(fg + 1) * 512], p[:], [p], [om])
                DMA("pool", out_ap[blk * 128:(blk + 1) * 128, :], om[:], [om], [])
    S.barrier()
    S.finish([])
    return nc


def prep_core_inputs(inputs, b, consts):
    f = lambda a: np.ascontiguousarray(np.asarray(a, dtype=np.float32))
    m = {}
    m["x"] = f(inputs["x"][b])
    m["cT"] = f(np.asarray(inputs["c"][b]).reshape(KC, 128).T)
    m["rel_bias"] = f(inputs["rel_bias"])
    m["ada_w"] = f(inputs["ada_w"])
    m["ada_bT"] = f(np.asarray(inputs["ada_b"]).reshape(2, 96, 128).transpose(2, 0, 1).reshape(128, 192))
    m["gmixT"] = f(np.asarray(inputs["norm_mix_g"]).reshape(2, KC, 128).transpose(2, 0, 1).reshape(128, 32))
    m["gffnT"] = f(np.asarray(inputs["norm_ffn_g"]).reshape(2, KC, 128).transpose(2, 0, 1).reshape(128, 32))
    m["ev_w_in"] = f(inputs["ev_w_in"][0])
    m["kvg_bc"] = f(np.broadcast_to(np.asarray(inputs["ev_kv_norm_g"][0])[None, :], (128, 256)))
    m["ev_w_uk"] = f(inputs["ev_w_uk"][0])
    m["ev_w_uv"] = f(inputs["ev_w_uv"][0])
    m["sinks_bc"] = f(np.broadcast_to(np.asarray(inputs["ev_sinks"][0])[None, :], (128, 16)))
    m["ev_w_out"] = f(inputs["ev_w_out"][0])
    m["od_w_in"] = f(inputs["od_w_in"][0])
    m["od_w_out"] = f(inputs["od_w_out"][0])
    m["ffn_w_up"] = f(inputs["ffn_w_up"])
    m["convwT"] = f(np.asarray(inputs["ffn_conv_w"]).reshape(2, 3, NFC, 128).transpose(3, 0, 1, 2).reshape(128, 2 * 3 * NFC))
    m["convbT"] = f(np.asarray(inputs["ffn_conv_b"]).reshape(2, NFC, 128).transpose(2, 0, 1).reshape(128, 2 * NFC))
    m["ffn_w_down"] = f(inputs["ffn_w_down"])
    m["finalgT"] = f(np.asarray(inputs["final_g"]).reshape(KC, 128).T)
    for k, v in consts.items():
        m["c_" + k] = v
    return m


def kernel(**inputs):
    nc = build_program()
    consts = host_constants()
    shared = None
    in_maps = []
    for core in range(N_CORES):
        b = core % 4
        m = prep_core_inputs(inputs, b, consts)
        if shared is None:
            shared = m
        else:
            for k in m:
                if k not in ("x", "cT"):
                    m[k] = shared[k]
        in_maps.append(m)
    res = run_bass_kernel_spmd(nc, in_maps, core_ids=list(range(N_CORES)))
    out = np.stack([np.asarray(res.results[b]["out"], dtype=np.float32) for b in range(4)], axis=0)
    return out
```

```python
from contextlib import ExitStack
import numpy as np
import concourse.bass as bass
import concourse.mybir as mybir
from concourse.bass_utils import run_bass_kernel_spmd

F32 = mybir.dt.float32
BF16 = mybir.dt.bfloat16
AF = mybir.ActivationFunctionType
ALU = mybir.AluOpType
AX = mybir.AxisListType


class Buf:
    def __init__(self, name, t):
        self.name = name
        self.t = t

    def __getitem__(self, idx):
        return self.t[idx]


class Sched:
    ENG = ("pe", "act", "dve", "pool", "sp")
    NRING = {"sp": ("sp", 12), "pool": ("pool", 8), "cast": ("pool", 16), "cast2": ("pool", 3)}

    def __init__(self, nc, same_engine_sync=True):
        self.nc = nc
        self.stack = ExitStack()
        self.same_engine_sync = same_engine_sync
        self.streams = {e: [] for e in self.ENG}
        self.cnt = {}
        self.esem = {}
        for e in ("pe", "act", "dve", "pool"):
            self.esem[e] = self.stack.enter_context(nc.semaphore("es_" + e))
            self.cnt[e] = 0
        self.ring = {}
        self.ring_tot = {}
        self.ring_i = {}
        for q, (_e, n) in self.NRING.items():
            self.ring[q] = [self.stack.enter_context(nc.semaphore(f"dq_{q}_{i}")) for i in range(n)]
            self.ring_tot[q] = [0] * n
            self.ring_i[q] = 0
        self.known = {e: {} for e in self.ENG}
        self.last_w = {}
        self.readers = {}
        self.semname = {}
        self.n_ops = 0

    def sbuf(self, name, shape, dtype):
        self.uid = getattr(self, "uid", 0) + 1
        name = f"{name}_{self.uid}"
        t = self.stack.enter_context(self.nc.sbuf_tensor(name, list(shape), dtype))
        return Buf(name, t)

    def psum(self, name, shape, dtype):
        self.uid = getattr(self, "uid", 0) + 1
        name = f"{name}_{self.uid}"
        t = self.stack.enter_context(self.nc.psum_tensor(name, list(shape), dtype))
        return Buf(name, t)

    @staticmethod
    def _key(r):
        return r if isinstance(r, (str, tuple)) else id(r)

    def _deps(self, eng, reads, writes):
        toks = []
        for r in reads:
            k = self._key(r)
            if k in self.last_w:
                toks.append(self.last_w[k])
        for w in writes:
            k = self._key(w)
            if k in self.last_w:
                toks.append(self.last_w[k])
            for sem_id, tok in self.readers.get(k, {}).items():
                toks.append(tok)
        best = {}
        for sem, v in toks:
            sid = id(sem)
            if sid not in best or best[sid][1] < v:
                best[sid] = (sem, v)
        waits = []
        own = self.esem.get(eng)
        for sid, (sem, v) in best.items():
            if own is not None and sem is own:
                if eng == "pe" or not self.same_engine_sync:
                    continue
            if self.known[eng].get(sid, 0) >= v:
                continue
            self.known[eng][sid] = v
            waits.append((sem, v))
        return waits

    def _commit(self, tok, reads, writes):
        for r in reads:
            k = self._key(r)
            d = self.readers.setdefault(k, {})
            sid = id(tok[0])
            if sid not in d or d[sid][1] < tok[1]:
                d[sid] = tok
        for w in writes:
            k = self._key(w)
            self.last_w[k] = tok
            self.readers[k] = {}

    def op(self, eng, fn, reads=(), writes=()):
        waits = self._deps(eng, reads, writes)
        self.cnt[eng] += 1
        tok = (self.esem[eng], self.cnt[eng])
        self.streams[eng].append((waits, fn, (self.esem[eng], 1)))
        self._commit(tok, reads, writes)
        self.n_ops += 1
        return tok

    def dma(self, q, out, in_, reads=(), writes=(), **kw):
        i = self.ring_i[q]
        self.ring_i[q] = (i + 1) % len(self.ring[q])
        sem = self.ring[q][i]
        prev = self.ring_tot[q][i]
        eng = self.NRING[q][0]
        waits = self._deps(eng, reads, writes)
        if prev > 0 and self.known[eng].get(id(sem), 0) < prev:
            self.known[eng][id(sem)] = prev
            waits.append((sem, prev))
        self.ring_tot[q][i] = prev + 16
        tok = (sem, prev + 16)
        self.streams[eng].append((waits, (lambda e: e.dma_start(out=out, in_=in_, **kw)), (sem, 16)))
        self._commit(tok, reads, writes)
        self.n_ops += 1
        return tok

    def dma_like(self, q, fn, reads=(), writes=()):
        i = self.ring_i[q]
        self.ring_i[q] = (i + 1) % len(self.ring[q])
        sem = self.ring[q][i]
        prev = self.ring_tot[q][i]
        eng = self.NRING[q][0]
        waits = self._deps(eng, reads, writes)
        if prev > 0 and self.known[eng].get(id(sem), 0) < prev:
            self.known[eng][id(sem)] = prev
            waits.append((sem, prev))
        self.ring_tot[q][i] = prev + 16
        tok = (sem, prev + 16)
        self.streams[eng].append((waits, fn, (sem, 16)))
        self._commit(tok, reads, writes)
        return tok

    def barrier(self):
        toks = [(self.esem[e], self.cnt[e]) for e in self.esem if self.cnt[e] > 0]
        for q in ("sp", "pool"):
            for sem, tot in zip(self.ring[q], self.ring_tot[q]):
                if tot > 0:
                    toks.append((sem, tot))
        for eng in self.ENG:
            waits = []
            for sem, v in toks:
                if sem is self.esem.get(eng):
                    continue
                if self.known[eng].get(id(sem), 0) >= v:
                    continue
                self.known[eng][id(sem)] = v
                waits.append((sem, v))
            if waits:
                self.streams[eng].append((waits, None, None))

    def finish(self, outputs):
        nc = self.nc
        final_waits = []
        for o in outputs:
            k = self._key(o)
            if k in self.last_w:
                final_waits.append(self.last_w[k])
        streams = self.streams
        engmap = {"pe": "tensor", "act": "scalar", "dve": "vector", "pool": "gpsimd", "sp": "sync"}

        def replay(name, e):
            for waits, fn, inc in streams[name]:
                for sem, v in waits:
                    e.wait_ge(sem, v)
                if fn is None:
                    continue
                ins = fn(e)
                ins.then_inc(inc[0], inc[1])
            if name == "sp":
                for sem, v in final_waits:
                    e.wait_ge(sem, v)

        with nc.Block() as block:
            for name in self.ENG:
                if not streams[name] and name != "sp":
                    continue
                getattr(block, engmap[name])(lambda e, name=name: replay(name, e))
        self.stack.close()


D = 2048
KC = 16
T = 4096
TT = 512
NT = T // TT
NB = T // 128
DFF = 5632
NFC = 88
EPS = 1e-6
NEG = -30000.0
NEGM_DT = mybir.dt.bfloat16
NEGM_V = -30000.0
N_CORES = 8


def _t5_bucket_np(rel):
    half, max_exact = 16, 8
    n = np.abs(rel)
    nf = np.maximum(n, 1).astype(np.float32)
    large = max_exact + (np.log(nf / max_exact) / np.log(np.float32(128 / max_exact)) * (half - max_exact)).astype(np.int32)
    large = np.minimum(large, half - 1)
    return np.where(rel > 0, half, 0) + np.where(n < max_exact, n, large)


def host_constants():
    c = {}
    c["ident"] = np.eye(128, dtype=np.float32)
    j = np.arange(128)
    c["negU"] = -(j[:, None] >= j[None, :]).astype(np.float32)
    c["Jmat"] = np.eye(128, dtype=np.float32)[::-1].copy()
    s = np.arange(128)[:, None]
    t = np.arange(512)[None, :]
    sb01 = np.zeros((128, 4, 512), np.float32)
    for r in range(4):
        sb01[:, r, :] = ((128 * r + s) < t)
    c["sb01"] = sb01.reshape(128, 2048)
    c["sbneg"] = ((1.0 - sb01) * NEG).reshape(128, 2048).astype(np.float32)
    tq = np.arange(128)[:, None]
    sk = np.arange(128)[None, :]
    c["idxdiag"] = np.where((sk // 64) <= (tq // 64), 0.0, -1e30).astype(np.float32)
    sp_ = np.arange(128)[:, None]
    tt_ = np.arange(128)[None, :]
    sw = np.zeros((128, 2, 128), np.float32)
    for di, delta in enumerate((0, -1)):
        kpos = 128 * (1 + delta) + (127 - sp_)
        qpos = 128 + tt_
        kc_, qc_ = kpos // 64, qpos // 64
        ok = (kc_ <= qc_) & (kc_ >= qc_ - 2)
        sw[:, di, :] = np.where(ok, 0.0, NEG)
    c["swamask"] = sw.reshape(128, 256)
    x = np.arange(384)
    relpos = 127 - x
    b = _t5_bucket_np(relpos.astype(np.int32))
    oh = np.zeros((32, 384), np.float32)
    oh[b, x] = 1.0
    oh[:, 383] = 0.0
    c["onehotR"] = oh
    k = np.arange(20, dtype=np.float64)
    c["pow2a"] = np.broadcast_to((2.0 ** -k)[None, :], (128, 20)).astype(np.float32).copy()
    c["pow2b"] = np.broadcast_to((2.0 ** -(k + 1))[None, :], (128, 20)).astype(np.float32).copy()
    return c


def build_program(debug=(), upto=99):
    nc = bass.Bass("TRN2", target_bir_lowering=False)
    S = Sched(nc)
    debug = set(debug)

    def din(name, shape, dt=F32):
        return nc.dram_tensor(name, list(shape), dt, kind="ExternalInput").ap()

    def dscr(name, shape, dt):
        kind = "ExternalOutput" if name in debug else "Internal"
        return nc.dram_tensor(name, list(shape), dt, kind=kind).ap()

    x_in = din("x", [T, D])
    cT_in = din("cT", [128, KC])
    relb_in = din("rel_bias", [32, 32])
    adaw_in = din("ada_w", [2, D, 6 * D])
    adab_in = din("ada_bT", [128, 192])
    gmix_in = din("gmixT", [128, 32])
    gffn_in = din("gffnT", [128, 32])
    evwin_in = din("ev_w_in", [D, 3400])
    kvg_in = din("kvg_bc", [128, 256])
    wuk_in = din("ev_w_uk", [16, 64, 256])
    wuv_in = din("ev_w_uv", [16, 256, 64])
    sinks_in = din("sinks_bc", [128, 16])
    evwout_in = din("ev_w_out", [D, D])
    odwin_in = din("od_w_in", [D, 3 * D])
    odwout_in = din("od_w_out", [D, D])
    wup_in = din("ffn_w_up", [2, D, 2 * DFF])
    convw_in = din("convwT", [128, 2 * 3 * NFC])
    convb_in = din("convbT", [128, 2 * NFC])
    wdn_in = din("ffn_w_down", [2, DFF, D])
    fing_in = din("finalgT", [128, KC])
    cst = {k: din("c_" + k, list(v.shape)) for k, v in host_constants().items()}
    out_ap = nc.dram_tensor("out", [T, D], F32, kind="ExternalOutput").ap()

    def MM(out, lhsT, rhs, start, stop, R, W):
        S.op("pe", lambda e: e.matmul(out, lhsT, rhs, start=start, stop=stop), R, W)

    def TR(out, in_, ident, R, W):
        S.op("pe", lambda e: e.transpose(out, in_, ident), R, W)

    def ACT(out, in_, func, R, W, bias=0.0, scale=1.0, accum=None):
        if accum is None:
            S.op("act", lambda e: e.activation(out, in_, func, bias=bias, scale=scale), R, W)
        else:
            S.op("act", lambda e: e.activation(out, in_, func, bias=bias, scale=scale, accum_out=accum), R, W)

    def TS(eng, out, in0, s1, s2, op0, op1, R, W):
        if s2 is None:
            S.op(eng, lambda e: e.tensor_scalar(out, in0, s1, None, op0), R, W)
        else:
            S.op(eng, lambda e: e.tensor_scalar(out, in0, s1, s2, op0, op1), R, W)

    def TTo(eng, out, in0, in1, op, R, W):
        S.op(eng, lambda e: e.tensor_tensor(out, in0, in1, op), R, W)

    def STT(eng, out, in0, sc, in1, op0, op1, R, W):
        S.op(eng, lambda e: e.scalar_tensor_tensor(out, in0, sc, in1, op0, op1), R, W)

    def CP(eng, out, in_, R, W):
        if eng == "act":
            S.op("act", lambda e: e.copy(out, in_), R, W)
        else:
            S.op(eng, lambda e: e.tensor_copy(out, in_), R, W)

    def MEMSET(eng, ap, val, W):
        S.op(eng, lambda e: e.memset(ap, val), (), W)

    def RECIP(out, in_, R, W):
        S.op("dve", lambda e: e.reciprocal(out, in_), R, W)

    def DMA(q, out, in_, R, W, **kw):
        S.dma(q, out, in_, R, W, **kw)

    evac_i = [0]

    def EVAC(out, in_, R, W, scale=None):
        evac_i[0] += 1
        if evac_i[0] % 2 == 0:
            if scale is None:
                CP("act", out, in_, R, W)
            else:
                ACT(out, in_, AF.Copy, R, W, scale=scale)
        else:
            if scale is None:
                CP("dve", out, in_, R, W)
            else:
                TS("dve", out, in_, scale, None, ALU.mult, None, R, W)

    def wt(name, nchunks, kc):
        return dscr(name, [nchunks, 128, kc, 128], BF16)

    W_evfm = wt("W_evfm", 25, KC)
    W_evtm = wt("W_evtm", 7, KC)
    W_evout = wt("W_evout", 16, KC)
    W_odin = wt("W_odin", 48, KC)
    W_odout = wt("W_odout", 16, KC)
    W_up = [wt(f"W_up{l}", NFC, KC) for l in range(2)]
    W_dn = [wt(f"W_dn{l}", 16, 44) for l in range(2)]
    xT = dscr("xT", [D, T], F32)
    qAT = dscr("qAT", [1024, T], BF16)
    kA2T = dscr("kA2T", [512, T], BF16)
    vA2 = dscr("vA2", [T, 512], BF16)
    qBT = dscr("qBT", [1024, T], BF16)
    ckv = dscr("ckv", [T, 256], BF16)
    ckvT = dscr("ckvT", [256, T], BF16)
    qiT = dscr("qiT", [512, T], BF16)
    kiT2 = dscr("kiT2", [128, T], BF16)
    wi_d = dscr("wi_d", [T, 8], F32)
    oT = dscr("oT", [D, T], BF16)
    q1T = dscr("q1T", [D, T], BF16)
    k1T = dscr("k1T", [D, T], BF16)
    v1 = dscr("v1", [T, D], BF16)
    gvec = dscr("gvec", [32, 384], F32)

    cast_q = ["cast"]
    cast_tasks = []

    def cast_cols(dst, n, src2d, c0, w, dcol, key, kc=KC):
        kper = 16
        for k0 in range(0, kc, kper):
            k1 = min(kc, k0 + kper)

            def go(k0=k0, k1=k1, q=cast_q[0]):
                DMA(q, dst[n, :, k0:k1, dcol:dcol + w],
                    src2d[k0 * 128:k1 * 128, c0:c0 + w].rearrange("(kc p) c -> p kc c", p=128),
                    [], [(key, n, k0)])
            if cast_q[0] == "cast":
                go()
            else:
                cast_tasks.append(go)

    def pump_casts(n=None):
        while cast_tasks and (n is None or n > 0):
            cast_tasks.pop(0)()
            if n is not None:
                n -= 1

    def wkeys(key, n, kc=KC):
        return [(key, n, k0) for k0 in range(0, kc, 16)]

    ev_splits = np.cumsum([0, 1024, 256, 256, 1024, 256, 512, 64, 8])
    o_qA, o_kA, o_vA, o_qB, o_cl, o_qi, o_ki, o_wi = [int(v) for v in ev_splits[:8]]
    for n in range(8):
        cast_cols(W_evfm, n, evwin_in, o_qA + n * 128, 128, 0, "W_evfm")
    for g in range(4):
        for hf in range(2):
            cast_cols(W_evfm, 8 + g, evwin_in, o_kA + g * 64, 64, hf * 64, "W_evfm")
    for n in range(8):
        cast_cols(W_evfm, 12 + n, evwin_in, o_qB + n * 128, 128, 0, "W_evfm")
    for n in range(4):
        cast_cols(W_evfm, 20 + n, evwin_in, o_qi + n * 128, 128, 0, "W_evfm")
    for hf in range(2):
        cast_cols(W_evfm, 24, evwin_in, o_ki, 64, hf * 64, "W_evfm")
    for g in range(4):
        for hf in range(2):
            cast_cols(W_evtm, g, evwin_in, o_vA + g * 64, 64, hf * 64, "W_evtm")
    cast_cols(W_evtm, 4, evwin_in, 3400 - 128, 128, 0, "W_evtm")
    for n in range(2):
        cast_cols(W_evtm, 5 + n, evwin_in, o_cl + n * 128, 128, 0, "W_evtm")
    ident = S.sbuf("ident", [128, 128], F32)
    identb = S.sbuf("identb", [128, 128], BF16)
    Jb = S.sbuf("Jb", [128, 128], BF16)
    onesb = S.sbuf("onesb", [128, 128], BF16)
    negonesb = S.sbuf("negonesb", [128, 128], BF16)
    negUb = S.sbuf("negUb", [128, 128], BF16)
    ctmp = S.sbuf("ctmp", [128, 128], F32)
    modv = S.sbuf("modv", [128, 192], F32)
    a1 = S.sbuf("a1", [128, 32], F32)
    a2 = S.sbuf("a2", [128, 32], F32)
    gmix = S.sbuf("gmix", [128, 32], F32)
    gffn = S.sbuf("gffn", [128, 32], F32)
    fing = S.sbuf("fing", [128, KC], F32)
    zero16 = S.sbuf("zero16", [128, KC], F32)
    convw = S.sbuf("convw", [128, 2 * 3 * NFC], F32)
    convb = S.sbuf("convb", [128, 2 * NFC], F32)
    epsb = S.sbuf("epsb", [128, 1], F32)
    DMA("sp", ident[:], cst["ident"], [], [ident])
    CP("dve", identb[:], ident[:], [ident], [identb])
    DMA("sp", ctmp[:], cst["Jmat"], [], [ctmp])
    CP("dve", Jb[:], ctmp[:], [ctmp], [Jb])
    DMA("sp", ctmp[:], cst["negU"], [Jb], [ctmp])
    CP("dve", negUb[:], ctmp[:], [ctmp], [negUb])
    MEMSET("dve", onesb[:], 1.0, [onesb])
    MEMSET("dve", negonesb[:], -1.0, [negonesb])
    MEMSET("dve", zero16[:], 0.0, [zero16])
    MEMSET("dve", epsb[:], EPS, [epsb])
    DMA("sp", gmix[:], gmix_in, [], [gmix])
    DMA("sp", gffn[:], gffn_in, [], [gffn])
    DMA("sp", fing[:], fing_in, [], [fing])
    DMA("sp", convw[:], convw_in, [], [convw])
    DMA("sp", convb[:], convb_in, [], [convb])

    class Scope:
        def __enter__(self):
            self.saved = S.stack
            S.stack = ExitStack()
            return self

        def __exit__(self, *a):
            S.barrier()
            S.stack.close()
            S.stack = self.saved
            return False

    with Scope():
        cact = S.sbuf("cact", [128, KC], F32)
        adab = S.sbuf("adab", [128, 192], F32)
        wblk = [S.sbuf(f"wblk{i}", [128, KC, 512], F32) for i in range(2)]
        psm = S.psum("psm", [128, 512], F32)
        psr = [S.psum(f"psr{i}", [128, 512], F32) for i in range(2)]
        rowsb = [S.sbuf(f"rowsb{i}", [1, 512], F32) for i in range(2)]
        one1 = S.sbuf("one1", [1, 1], F32)
        MEMSET("dve", one1[:], 1.0, [one1])
        DMA("sp", cact[:], cT_in, [], [cact])
        DMA("sp", adab[:], adab_in, [], [adab])
        ACT(cact[:], cact[:], AF.Silu, [cact], [cact])
        bi = 0
        for l in range(2):
            for cb in range(24):
                wb = wblk[bi % 2]
                pr = psr[bi % 2]
                rs = rowsb[bi % 2]
                bi += 1
                DMA("sp", wb[:], adaw_in[l][:, cb * 512:(cb + 1) * 512].rearrange("(kc p) n -> p kc n", p=128), [], [wb])
                for kc in range(KC):
                    MM(pr[0:1, :], cact[:, kc:kc + 1], wb[:, kc, :], kc == 0, kc == KC - 1, [wb, cact], [pr])
                EVAC(rs[:], pr[0:1, :], [pr], [rs])
                for jj in range(4):
                    j = cb * 4 + jj
                    MM(psm[:, l * 96 + j:l * 96 + j + 1], rs[0:1, jj * 128:(jj + 1) * 128], one1[0:1, 0:1], True, True, [rs, one1], [psm])
        TTo("dve", modv[:], psm[:, 0:192], adab[:], ALU.add, [psm, adab], [modv])
        for l in range(2):
            STT("dve", a1[:, l * 16:(l + 1) * 16], modv[:, l * 96 + 16:l * 96 + 32], 1.0, gmix[:, l * 16:(l + 1) * 16],
                ALU.add, ALU.mult, [modv, gmix], [a1])
            STT("dve", a2[:, l * 16:(l + 1) * 16], modv[:, l * 96 + 64:l * 96 + 80], 1.0, gffn[:, l * 16:(l + 1) * 16],
                ALU.add, ALU.mult, [modv, gffn], [a2])

    cast_q[0] = "cast2"
    for n in range(16):
        cast_cols(W_evout, n, evwout_in, n * 128, 128, 0, "W_evout")
    for n in range(NFC):
        cast_cols(W_up[0], n, wup_in[0], n * 128, 128, 0, "W_up0")
    for n in range(16):
        cast_cols(W_dn[0], n, wdn_in[0], n * 128, 128, 0, "W_dn0", kc=44)
    for n in range(48):
        cast_cols(W_odin, n, odwin_in, n * 128, 128, 0, "W_odin")
    for n in range(16):
        cast_cols(W_odout, n, odwout_in, n * 128, 128, 0, "W_odout")
    for n in range(NFC):
        cast_cols(W_up[1], n, wup_in[1], n * 128, 128, 0, "W_up1")
    for n in range(16):
        cast_cols(W_dn[1], n, wdn_in[1], n * 128, 128, 0, "W_dn1", kc=44)


    def mv(l, grp, fc):
        c = l * 96 + grp * 16 + fc
        return modv[:, c:c + 1]

    def norm_tile(xt, hT, sq, tmpf, pss, rstd, gain, shift, outf=None):
        ACT(sq[:, 0:KC, :], xt[:], AF.Square, [xt], [sq])
        for kc in range(KC):
            MM(pss[:], onesb[:], sq[:, kc, :], kc == 0, kc == KC - 1, [sq, onesb], [pss])
        ACT(rstd[:], pss[:], AF.Sqrt, [pss], [rstd], bias=epsb[:, 0:1], scale=1.0 / D)
        RECIP(rstd[:], rstd[:], [rstd], [rstd])
        for fc in range(KC):
            if shift is None:
                STT("dve", outf[:, fc, :], xt[:, fc, :], gain(fc), rstd[:], ALU.mult, ALU.mult, [xt, rstd], [(outf.name, fc)])
                continue
            tf = tmpf[fc % 2]
            STT("dve", tf[:], xt[:, fc, :], gain(fc), rstd[:], ALU.mult, ALU.mult, [xt, rstd], [tf])
            ACT(hT[:, fc, :], tf[:], AF.Identity, [tf], [(hT.name, fc)], bias=shift(fc), scale=1.0)

    def hkeys(hT):
        return [(hT.name, fc) for fc in range(KC)]

    class WRing:
        def __init__(self, name, n, kc):
            self.bufs = [S.sbuf(f"{name}{i}", [128, kc, 128], BF16) for i in range(n)]
            self.i = 0
            self.kc = kc

        def load(self, Wt, n, key):
            b = self.bufs[self.i % len(self.bufs)]
            self.i += 1
            DMA("sp", b[:], Wt[n], wkeys(key, n, self.kc), [b])
            return b

    xT_v = xT.rearrange("(fc p) t -> p fc t", p=128)
    oT_v = oT.rearrange("(fc p) t -> p fc t", p=128)

    if upto >= 1:
      with Scope():
        xtm = [S.sbuf(f"xtm{i}", [128, D], F32) for i in range(4)]
        xt = S.sbuf("xt", [128, KC, TT], F32)
        sq = S.sbuf("sq", [128, KC, TT], BF16)
        hT = S.sbuf("hT", [128, KC, TT], BF16)
        tmpf = [S.sbuf(f"tmpf{i}", [128, TT], F32) for i in range(2)]
        rstd = S.sbuf("rstd", [128, TT], F32)
        wtm = S.sbuf("wtm", [128, KC, 7 * 128], BF16)
        kvg = S.sbuf("kvg", [128, 256], F32)
        stg = [S.sbuf(f"stg{i}", [128, TT], BF16) for i in range(3)]
        junk = S.sbuf("junk", [128, 256], F32)
        ss = S.sbuf("ss", [128, 1], F32)
        cstg = S.sbuf("cstg", [128, 256], BF16)
        ctstg = S.sbuf("ctstg", [128, 2, 128], BF16)
        wistg = S.sbuf("wistg", [128, 8], F32)
        wr = WRing("wr", 6, KC)
        pst = [S.psum(f"pst{i}", [128, TT], F32) for i in range(2)]
        pss = S.psum("pss", [128, TT], F32)
        psg = [S.psum(f"psg{i}", [128, TT], F32) for i in range(2)]
        psA = S.psum("psA", [128, TT], F32)
        psB = S.psum("psB", [128, TT], F32)
        pstb = S.psum("pstb", [128, 128], BF16)
        for n in range(7):
            DMA("sp", wtm[:, :, n * 128:(n + 1) * 128], W_evtm[n], wkeys("W_evtm", n), [wtm])
        DMA("sp", kvg[:], kvg_in, [], [kvg])
        fm_dst = [(qAT, n) for n in range(8)] + [(kA2T, n) for n in range(4)] + [(qBT, n) for n in range(8)] + \
                 [(qiT, n) for n in range(4)] + [(kiT2, 0)]
        si = 0
        for tt in range(NT):
            tsl = slice(tt * TT, (tt + 1) * TT)
            for tb in range(4):
                blk = tt * 4 + tb
                DMA("sp", xtm[tb][:], x_in[blk * 128:(blk + 1) * 128, :], [], [xtm[tb]])
            for fc in range(KC):
                p = pst[fc % 2]
                for tb in range(4):
                    TR(p[:, tb * 128:(tb + 1) * 128], xtm[tb][:, fc * 128:(fc + 1) * 128], ident[:], [xtm[tb], ident], [p])
                EVAC(xt[:, fc, :], p[:], [p], [xt])
            DMA("pool", xT_v[:, :, tsl], xt[:], [xt], [])
            pump_casts(4)
            norm_tile(xt, hT, sq, tmpf, pss, rstd, lambda fc: a1[:, fc:fc + 1], lambda fc: mv(0, 0, fc))
            for n in range(25):
                wb = wr.load(W_evfm, n, "W_evfm")
                p = psg[n % 2]
                for kc in range(KC):
                    MM(p[:], wb[:, kc, :], hT[:, kc, :], kc == 0, kc == KC - 1, [wb] + hkeys(hT), [p])
                sb = stg[si % 3]
                si += 1
                EVAC(sb[:], p[:], [p], [sb])
                dst, dn = fm_dst[n]
                DMA("pool", dst[dn * 128:(dn + 1) * 128, tsl], sb[:], [sb], [])
            for tb in range(4):
                blk = tt * 4 + tb
                bsl = slice(blk * 128, (blk + 1) * 128)
                for kc in range(KC):
                    MM(psA[:], hT[:, kc, tb * 128:(tb + 1) * 128], wtm[:, kc, 0:512], kc == 0, kc == KC - 1, [wtm] + hkeys(hT), [psA])
                for kc in range(KC):
                    MM(psB[:, 0:264], hT[:, kc, tb * 128:(tb + 1) * 128], wtm[:, kc, 632:896], kc == 0, kc == KC - 1, [wtm] + hkeys(hT), [psB])
                sb = stg[si % 3]
                si += 1
                EVAC(sb[:], psA[:], [psA], [sb])
                DMA("pool", vA2[bsl, :], sb[:], [sb], [])
                ACT(junk[:], psB[:, 8:264], AF.Square, [psB], [junk, ss], accum=ss[:])
                ACT(ss[:], ss[:], AF.Sqrt, [ss], [ss], bias=epsb[:, 0:1], scale=1.0 / 256)
                RECIP(ss[:], ss[:], [ss], [ss])
                STT("dve", cstg[:], psB[:, 8:264], ss[:, 0:1], kvg[:], ALU.mult, ALU.mult, [psB, ss, kvg], [cstg])
                TS("dve", wistg[:], psB[:, 0:8], float(8 ** -0.5 * 64 ** -0.5), None, ALU.mult, None, [psB], [wistg])
                DMA("pool", ckv[bsl, :], cstg[:], [cstg], [])
                DMA("pool", wi_d[bsl, :], wistg[:], [wistg], [])
                for rc in range(2):
                    TR(pstb[:], cstg[:, rc * 128:(rc + 1) * 128], identb[:], [cstg, identb], [pstb])
                    EVAC(ctstg[:, rc, :], pstb[:], [pstb], [ctstg])
                DMA("pool", ckvT.rearrange("(rc p) t -> p rc t", p=128)[:, :, bsl], ctstg[:], [ctstg], [])
    if upto >= 2:
      with Scope():
        relb = S.sbuf("relb", [32, 32], F32)
        ohr = S.sbuf("ohr", [32, 384], F32)
        gsb = S.sbuf("gsb", [32, 384], F32)
        psb_ = S.psum("psb_", [128, TT], F32)
        DMA("sp", relb[:], relb_in, [], [relb])
        DMA("sp", ohr[:], cst["onehotR"], [], [ohr])
        MM(psb_[0:32, 0:384], relb[:], ohr[:], True, True, [relb, ohr], [psb_])
        CP("dve", gsb[:], psb_[0:32, 0:384], [psb_], [gsb])
        DMA("pool", gvec, gsb[:], [gsb], [])

    def toeplitz_src(h):
        return bass.AP(tensor=gvec.tensor, offset=gvec.offset + h * 384, ap=[[1, 128], [128, 2], [1, 128]])

    if upto >= 2:
      with Scope():
        biasA = S.sbuf("biasA", [128, 16, 2, 128], BF16)
        btmp = [S.sbuf(f"btmp{i}", [128, 2, 128], F32) for i in range(2)]
        swm = S.sbuf("swm", [128, 2, 128], F32)
        esink = S.sbuf("esink", [128, 16], F32)
        qa = S.sbuf("qa", [128, 8, TT], BF16)
        ka = S.sbuf("ka", [128, 4, 640], BF16)
        va = S.sbuf("va", [128, 5, 512], BF16)
        Pb = [S.sbuf(f"Pb{i}", [128, 512], BF16) for i in range(2)]
        den = [S.sbuf(f"den{i}", [128, 128], F32) for i in range(2)]
        ostg = S.sbuf("ostg", [128, 8, TT], BF16)
        psS = [S.psum(f"psS{i}", [128, 512], F32) for i in range(2)]
        psO = [S.psum(f"psO{i}", [128, 512], F32) for i in range(2)]
        DMA("sp", swm[:], cst["swamask"].rearrange("p (a t) -> p a t", a=2), [], [swm])
        DMA("sp", esink[:], sinks_in, [], [esink])
        ACT(esink[:], esink[:], AF.Exp, [esink], [esink])
        for h in range(16):
            bt = btmp[h % 2]
            DMA("sp", bt[:], toeplitz_src(h), [], [bt])
            STT("dve", biasA[:, h, :, :], bt[:], 8.0, swm[:], ALU.mult, ALU.add, [bt, swm], [biasA])
        it = 0
        for qt in range(NT):
            tsl = slice(qt * TT, (qt + 1) * TT)
            DMA("sp", qa[:], qAT.rearrange("(c p) t -> p c t", p=128)[:, :, tsl], [], [qa])
            if qt == 0:
                DMA("sp", ka[:, :, 128:640], kA2T.rearrange("(g p) t -> p g t", p=128)[:, :, 0:512], [], [ka])
                DMA("sp", va[:, 1:5, :], vA2[0:512, :].rearrange("(b s) c -> s b c", s=128), [], [va])
            else:
                DMA("sp", ka[:], kA2T.rearrange("(g p) t -> p g t", p=128)[:, :, qt * TT - 128:(qt + 1) * TT], [], [ka])
                DMA("sp", va[:], vA2[qt * TT - 128:(qt + 1) * TT, :].rearrange("(b s) c -> s b c", s=128), [], [va])
            for j in range(4):
                qb = qt * 4 + j
                dis = (0,) if qb == 0 else (0, 1)
                for c in range(8):
                    g = c // 2
                    Sb = psS[it % 2]
                    Ob = psO[it % 2]
                    P = Pb[it % 2]
                    it += 1
                    for e in range(2):
                        h = 2 * c + e
                        hs = slice(e * 64, (e + 1) * 64)
                        for di in dis:
                            kbi = 1 + j - di
                            slot = e * 2 + di
                            MM(Sb[:, slot * 128:(slot + 1) * 128], ka[hs, g, kbi * 128:(kbi + 1) * 128], qa[hs, c, j * 128:(j + 1) * 128],
                               True, False, [ka, qa], [Sb])
                            MM(Sb[:, slot * 128:(slot + 1) * 128], Jb[:], biasA[:, h, di, :], False, True, [Jb, biasA], [Sb])
                    if qb == 0:
                        for e in range(2):
                            ACT(P[:, e * 256:e * 256 + 128], Sb[:, e * 256:e * 256 + 128], AF.Exp, [Sb], [P], scale=0.125)
                    else:
                        ACT(P[:], Sb[:], AF.Exp, [Sb], [P], scale=0.125)
                    for e in range(2):
                        for di in dis:
                            kbi = 1 + j - di
                            slot = e * 2 + di
                            MM(Ob[:, e * 128:(e + 1) * 128], va[:, kbi, g * 128:(g + 1) * 128], P[:, slot * 128:(slot + 1) * 128],
                               di == dis[0], di == dis[-1], [va, P], [Ob])
                    for e in range(2):
                        for di in dis:
                            slot = e * 2 + di
                            MM(Ob[:, (2 + e) * 128:(3 + e) * 128], onesb[:], P[:, slot * 128:(slot + 1) * 128],
                               di == dis[0], di == dis[-1], [onesb, P], [Ob])
                    for e in range(2):
                        h = 2 * c + e
                        hs = slice(e * 64, (e + 1) * 64)
                        dn = den[e]
                        TS("dve", dn[hs, :], Ob[hs, (2 + e) * 128:(3 + e) * 128], esink[hs, h:h + 1], None, ALU.add, None, [Ob, esink], [dn])
                        RECIP(dn[hs, :], dn[hs, :], [dn], [dn])
                        TTo("dve", ostg[hs, c, j * 128:(j + 1) * 128], Ob[hs, e * 128:(e + 1) * 128], dn[hs, :], ALU.mult, [Ob, dn], [ostg])
            DMA("pool", oT_v[:, 0:8, tsl], ostg[:], [ostg], [])
            pump_casts(4)

    if upto >= 3:
      with Scope():
        NBIS = 20
        ki2 = S.sbuf("ki2", [128, T], BF16)
        ckvT_sb = S.sbuf("ckvT_sb", [128, 2, T], BF16)
        ckv_sb = S.sbuf("ckv_sb", [128, NB, 256], BF16)
        wuk_sb = S.sbuf("wuk_sb", [128, 8, 256], BF16)
        wuv2 = S.sbuf("wuv2", [128, 16, 2, 128], BF16)
        biasB = S.sbuf("biasB", [128, 16, 2, 128], BF16)
        farb = S.sbuf("farb", [128, 16], F32)
        idg = S.sbuf("idg", [128, 128], F32)
        p2a = S.sbuf("p2a", [128, NBIS], F32)
        p2b = S.sbuf("p2b", [128, NBIS], F32)
        with Scope():
            wukf = S.sbuf("wukf", [128, 8, 256], F32)
            wuvf = S.sbuf("wuvf", [128, 16, 2, 64], F32)
            btmp = [S.sbuf(f"btmpd{i}", [128, 2, 128], F32) for i in range(2)]
            DMA("sp", wukf[:], wuk_in.rearrange("(c e) d r -> (e d) c r", e=2), [], [wukf])
            CP("dve", wuk_sb[:], wukf[:], [wukf], [wuk_sb])
            DMA("sp", wuvf[:], wuv_in.rearrange("h (rc r) d -> r h rc d", r=128), [], [wuvf])
            CP("dve", wuv2[:, :, :, 0:64], wuvf[:], [wuvf], [wuv2])
            CP("dve", wuv2[:, :, :, 64:128], wuvf[:], [wuvf], [wuv2])
            DMA("sp", farb[:], bass.AP(tensor=gvec.tensor, offset=gvec.offset + 16 * 384 + 382, ap=[[0, 128], [384, 16]]), [], [farb],
                allow_slow_non_contiguous=True)
            for h in range(16):
                bt = btmp[h % 2]
                DMA("sp", bt[:], toeplitz_src(16 + h), [], [bt])
                TS("dve", biasB[:, h, :, :], bt[:], farb[:, h:h + 1], None, ALU.subtract, None, [bt, farb], [biasB])
        score = [S.sbuf(f"score{i}", [128, T], F32) for i in range(2)]
        negm = [[S.sbuf(f"negm{p}_{i}", [128, T], NEGM_DT) for i in range(4)] for p in range(2)]
        rbuf = [S.sbuf(f"rbuf{i}", [128, 512], F32) for i in range(2)]
        bis = [S.sbuf(f"bis{i}", [128, 8], F32) for i in range(2)]
        htab = [S.sbuf(f"htab{i}", [128, 2, NBIS], F32) for i in range(2)]
        qi = S.sbuf("qi", [128, 4, TT], BF16)
        wi4 = S.sbuf("wi4", [128, 4, 8], F32)
        qbt = S.sbuf("qbt", [128, 8, TT], BF16)
        ql = [S.sbuf(f"ql{i}", [128, 2, TT], BF16) for i in range(2)]
        Pb = [S.sbuf(f"Pd{i}", [128, 512], BF16) for i in range(3)]
        rec = [S.sbuf(f"rec{i}", [128, 512], F32) for i in range(2)]
        olat = [S.sbuf(f"olat{i}", [128, 2, 512], BF16) for i in range(2)]
        ostg = [S.sbuf(f"ostgd{i}", [128, TT], BF16) for i in range(2)]
        ounn = [S.sbuf(f"ounn{i}", [128, TT], F32) for i in range(2)]
        psi = [S.psum(f"psi{i}", [128, 512], F32) for i in range(2)]
        psq = S.psum("psq", [128, 512], F32)
        pss2 = [S.psum(f"pss2{i}", [128, 512], F32) for i in range(2)]
        Oacc = [S.psum(f"Oacc{i}", [128, 512], F32) for i in range(2)]
        Dacc = S.psum("Dacc", [128, 512], F32)
        DMA("sp", ki2[:], kiT2, [], [ki2])
        DMA("sp", ckvT_sb[:], ckvT.rearrange("(rc p) t -> p rc t", p=128), [], [ckvT_sb])
        DMA("sp", ckv_sb[:], ckv.rearrange("(b s) r -> s b r", s=128), [], [ckv_sb])
        DMA("sp", idg[:], cst["idxdiag"], [], [idg])
        DMA("sp", p2a[:], cst["pow2a"], [], [p2a])
        DMA("sp", p2b[:], cst["pow2b"], [], [p2b])
        ri = [0]
        sci = [0]
        bis_tasks = []

        def index_qblock(qt, j):
            tsl = slice(qt * TT, (qt + 1) * TT)
            nkb = 4 * qt + 4
            if j == 0:
                DMA("sp", qi[:], qiT.rearrange("(c p) t -> p c t", p=128)[:, :, tsl], [], [qi])
                DMA("sp", wi4[:], wi_d[tsl, :].rearrange("(j t) h -> t j h", t=128), [], [wi4])
            if True:
                qb = qt * 4 + j
                nk = (qb + 1) * 128
                sc = score[sci[0] % 2]
                bs = bis[sci[0] % 2]
                ht = htab[sci[0] % 2]
                sci[0] += 1
                nm = negm[qt % 2][j]
                for h in range(8):
                    c, e = h // 2, h % 2
                    hs = slice(e * 64, (e + 1) * 64)
                    for k0 in range(0, nk, 512):
                        w = min(512, nk - k0)
                        p = psi[ri[0] % 2]
                        rb = rbuf[ri[0] % 2]
                        ri[0] += 1
                        MM(p[:, 0:w], qi[hs, c, j * 128:(j + 1) * 128], ki2[hs, k0:k0 + w], True, True, [qi, ki2], [p])
                        ACT(rb[:, 0:w], p[:, 0:w], AF.Relu, [p], [rb])
                        sk = (sc.name, k0)
                        if h == 0:
                            TS("dve", sc[:, k0:k0 + w], rb[:, 0:w], wi4[:, j, 0:1], None, ALU.mult, None, [rb, wi4], [sk, sc])
                        else:
                            STT("dve", sc[:, k0:k0 + w], rb[:, 0:w], wi4[:, j, h:h + 1], sc[:, k0:k0 + w], ALU.mult, ALU.add,
                                [rb, wi4, sk], [sk])
                allk = [(sc.name, k0) for k0 in range(0, nk, 512)]
                S.op("dve", lambda e_, sc=sc, bs=bs, nk=nk: e_.tensor_reduce(bs[:, 0:1], sc[:, 0:nk], AX.X, ALU.max, apply_absolute_value=True),
                     allk, [bs])
                TS("dve", bs[:, 0:1], bs[:, 0:1], 1.0, None, ALU.add, None, [bs], [bs])
                TS("dve", ht[:, 0, :], p2a[:], bs[:, 0:1], None, ALU.mult, None, [bs, p2a], [ht])
                TS("dve", ht[:, 1, :], p2b[:], bs[:, 0:1], None, ALU.mult, None, [bs, p2b], [ht])
                MEMSET("dve", bs[:, 1:2], 0.0, [bs])
                TTo("dve", sc[:, qb * 128:(qb + 1) * 128], sc[:, qb * 128:(qb + 1) * 128], idg[:], ALU.add, allk + [idg, bs], [sc])

                def bis_iter(k, sc=sc, bs=bs, ht=ht, nk=nk, nm=nm):
                    S.op("dve", lambda e_: e_.tensor_scalar(nm[:, 0:nk], sc[:, 0:nk], bs[:, 1:2], 0.0, ALU.is_ge, ALU.add,
                                                            accum_out=bs[:, 2:3]), [sc, bs], [nm, bs])
                    TS("dve", bs[:, 4:5], bs[:, 1:2], ht[:, 1, k:k + 1], None, ALU.subtract, None, [bs, ht], [bs])
                    STT("dve", bs[:, 1:2], bs[:, 2:3], 255.5, ht[:, 0, k:k + 1], ALU.is_ge, ALU.mult, [bs, ht], [bs])
                    TTo("dve", bs[:, 1:2], bs[:, 1:2], bs[:, 4:5], ALU.add, [bs], [bs])

                def bis_final(sc=sc, bs=bs, ht=ht, nk=nk, nm=nm, nkb=nkb):
                    TS("dve", bs[:, 5:6], bs[:, 1:2], ht[:, 1, NBIS - 1:NBIS], None, ALU.subtract, None, [bs, ht], [bs])
                    S.op("dve", lambda e_: e_.tensor_scalar(nm[:, 0:nk], sc[:, 0:nk], bs[:, 5:6], NEGM_V, ALU.is_lt, ALU.mult), [sc, bs], [nm])
                    if nk < nkb * 128:
                        MEMSET("pool", nm[:, nk:nkb * 128], NEGM_V, [nm])

                for k in range(NBIS):
                    bis_tasks.append((qb, lambda k=k, f=bis_iter: f(k)))
                bis_tasks.append((qb, bis_final))

        def run_tasks(n=None, older_than=None):
            while bis_tasks:
                if older_than is not None and bis_tasks[0][0] >= older_than:
                    break
                if n is not None:
                    if n <= 0:
                        break
                    n -= 1
                bis_tasks.pop(0)[1]()

        def stage_attn(qt):
            tsl = slice(qt * TT, (qt + 1) * TT)
            nkb = 4 * qt + 4
            nmt = negm[qt % 2]
            DMA("sp", qbt[:], qBT.rearrange("(c p) t -> p c t", p=128)[:, :, tsl], [], [qbt])
            ditems = [(h, kb) for h in range(16) for kb in range(nkb)]

            def dstageA(i):
                h, kb = ditems[i]
                c, e = h // 2, h % 2
                hs = slice(e * 64, (e + 1) * 64)
                q_ = ql[h % 2]
                if kb == 0:
                    for rc in range(2):
                        MM(psq[:], wuk_sb[hs, c, rc * 128:(rc + 1) * 128], qbt[hs, c, :], True, True, [wuk_sb, qbt], [psq])
                        ACT(q_[:, rc, :], psq[:], AF.Copy, [psq], [q_], scale=0.125)
                ksl = slice(kb * 128, (kb + 1) * 128)
                Sb = pss2[i % 2]
                P = Pb[i % 3]
                ops = [(Sb[:], ckvT_sb[:, 0, ksl], q_[:, 0, :], [ckvT_sb, q_]),
                       (Sb[:], ckvT_sb[:, 1, ksl], q_[:, 1, :], [ckvT_sb, q_])]
                for j in range(4):
                    qb = qt * 4 + j
                    cs = slice(j * 128, (j + 1) * 128)
                    ops.append((Sb[:, cs], nmt[j][:, ksl], identb[:], [nmt[j], identb]))
                    if kb == qb:
                        ops.append((Sb[:, cs], Jb[:], biasB[:, h, 0, :], [Jb, biasB]))
                    if kb == qb - 1:
                        ops.append((Sb[:, cs], Jb[:], biasB[:, h, 1, :], [Jb, biasB]))
                for oi, (o_, l_, r_, rd) in enumerate(ops):
                    MM(o_, l_, r_, oi == 0, oi == len(ops) - 1, rd, [Sb])
                ACT(P[:], Sb[:], AF.Exp, [Sb, farb], [P], bias=farb[:, h:h + 1], scale=1.0)

            def dstageB(i):
                h, kb = ditems[i]
                c, e = h // 2, h % 2
                hs = slice(e * 64, (e + 1) * 64)
                P = Pb[i % 3]
                MM(Oacc[0][:], ckv_sb[:, kb, 0:128], P[:], kb == 0, kb == nkb - 1, [ckv_sb, P], [Oacc[0]])
                MM(Oacc[1][:], ckv_sb[:, kb, 128:256], P[:], kb == 0, kb == nkb - 1, [ckv_sb, P], [Oacc[1]])
                MM(Dacc[:], onesb[:], P[:], kb == 0, kb == nkb - 1, [onesb, P], [Dacc])
                if kb == nkb - 1:
                    og = ostg[c % 2]
                    ol = olat[h % 2]
                    rc_ = rec[h % 2]
                    for rc in range(2):
                        CP("act", ol[:, rc, :], Oacc[rc][:], [Oacc[rc]], [ol])
                    CP("act", rc_[hs, :], Dacc[hs, :], [Dacc], [rc_])
                    RECIP(rc_[hs, :], rc_[hs, :], [rc_], [rc_])
                    MM(psq[:], wuv2[:, h, 0, :], ol[:, 0, :], True, False, [wuv2, ol], [psq])
                    MM(psq[:], wuv2[:, h, 1, :], ol[:, 1, :], False, True, [wuv2, ol], [psq])
                    ou = ounn[h % 2]
                    CP("act", ou[hs, :], psq[hs, :], [psq], [ou])
                    TTo("pool", og[hs, :], ou[hs, :], rc_[hs, :], ALU.mult, [ou, rc_], [og])
                    if e == 1:
                        DMA("pool", oT[(8 + c) * 128:(9 + c) * 128, tsl], og[:], [og], [])

            nd = len(ditems)
            rate = 2 if nd < 160 else 1
            marks = {(nd * (jn + 1)) // 5: jn for jn in range(4)}
            for i in range(nd + 1):
                if i < nd:
                    dstageA(i)
                if i >= 1:
                    dstageB(i - 1)
                run_tasks(n=rate)
                if i % 4 == 0:
                    pump_casts(1)
                if i in marks and qt + 1 < NT:
                    run_tasks(older_than=(qt + 1) * 4 + marks[i] - 1)
                    index_qblock(qt + 1, marks[i])
            run_tasks()

        for j in range(4):
            index_qblock(0, j)
            run_tasks()
        for qt in range(NT):
            stage_attn(qt)

    def out_ffn_phase(l, Wout, wkey):
        with Scope():
            xt = S.sbuf("xt", [128, KC, TT], F32)
            ot = S.sbuf("ot", [128, KC, TT], BF16)
            ubig = S.sbuf("ubig", [128, 44, TT], BF16)
            tmpf = [S.sbuf(f"tmpf{i}", [128, TT], F32) for i in range(2)]
            rstd = S.sbuf("rstd", [128, TT], F32)
            ub = [S.sbuf(f"ub{i}", [128, TT + 2], F32) for i in range(2)]
            acc = [S.sbuf(f"acc{i}", [128, TT], F32) for i in range(2)]
            carry = S.sbuf("carry", [128, NFC, 2], F32)
            wr = WRing("wr", 6, KC)
            wrd = WRing("wrd", 2, 44)
            psg = [S.psum(f"psg{i}", [128, TT], F32) for i in range(2)]
            pss = S.psum("pss", [128, TT], F32)
            psu = [S.psum(f"psu{i}", [128, TT], F32) for i in range(2)]
            psd = [S.psum(f"psd{i}", [128, TT], F32) for i in range(2)]
            cwo = l * 3 * NFC
            for tt in range(NT):
                tsl = slice(tt * TT, (tt + 1) * TT)
                DMA("sp", xt[:], xT_v[:, :, tsl], [], [xt] + hkeys(xt))
                DMA("sp", ot[:], oT_v[:, :, tsl], [], [ot] + hkeys(ot))
                for n in range(KC):
                    wb = wr.load(Wout, n, wkey)
                    p = psg[n % 2]
                    for kc in range(KC):
                        MM(p[:], wb[:, kc, :], ot[:, kc, :], kc == 0, kc == KC - 1, [wb, ot], [p])
                    STT("dve", xt[:, n, :], p[:], mv(l, 2, n), xt[:, n, :], ALU.mult, ALU.add, [p, (xt.name, n)], [(xt.name, n)])
                S.op("pool", lambda e_: e_.engine_nop(), hkeys(xt), [xt])
                norm_tile(xt, ot, ubig, tmpf, pss, rstd, lambda fc: a2[:, l * 16 + fc:l * 16 + fc + 1], lambda fc: mv(l, 3, fc))
                S.op("pool", lambda e_: e_.engine_nop(), hkeys(ot), [ot])
                for j in range(44):
                    for which, cidx in ((0, j), (1, 44 + j)):
                        wb = wr.load(W_up[l], cidx, f"W_up{l}")
                        p = psu[which]
                        u = ub[which]
                        a = acc[which]
                        for kc in range(KC):
                            MM(p[:], wb[:, kc, :], ot[:, kc, :], kc == 0, kc == KC - 1, [wb, ot], [p])
                        CP("act", u[:, 2:TT + 2], p[:], [p], [u])
                        if tt == 0:
                            MEMSET("pool", u[:, 0:2], 0.0, [u])
                        else:
                            CP("pool", u[:, 0:2], carry[:, cidx, :], [(carry.name, cidx)], [u])
                        ACT(a[:], p[:], AF.Identity, [p], [a], bias=convb[:, l * NFC + cidx:l * NFC + cidx + 1],
                            scale=convw[:, cwo + 2 * NFC + cidx:cwo + 2 * NFC + cidx + 1])
                        STT("dve", a[:], u[:, 1:TT + 1], convw[:, cwo + NFC + cidx:cwo + NFC + cidx + 1], a[:], ALU.mult, ALU.add, [u, a], [a])
                        STT("dve", a[:], u[:, 0:TT], convw[:, cwo + cidx:cwo + cidx + 1], a[:], ALU.mult, ALU.add, [u, a], [a])
                        CP("pool", carry[:, cidx, :], u[:, TT:TT + 2], [u], [(carry.name, cidx)])
                    ACT(acc[0][:], acc[0][:], AF.Silu, [acc[0]], [acc[0]])
                    TTo("dve", ubig[:, j, :], acc[0][:], acc[1][:], ALU.mult, [acc[0], acc[1]], [(ubig.name, j)])
                S.op("pool", lambda e_: e_.engine_nop(), [(ubig.name, j) for j in range(44)], [ubig])
                for n in range(KC):
                    wd = wrd.load(W_dn[l], n, f"W_dn{l}")
                    p = psd[n % 2]
                    for kc in range(44):
                        MM(p[:], wd[:, kc, :], ubig[:, kc, :], kc == 0, kc == 43, [wd, ubig], [p])
                    STT("dve", xt[:, n, :], p[:], mv(l, 5, n), xt[:, n, :], ALU.mult, ALU.add, [p, (xt.name, n)], [(xt.name, n)])
                S.op("pool", lambda e_: e_.engine_nop(), hkeys(xt), [xt])
                DMA("pool", xT_v[:, :, tsl], xt[:], [xt], [])

    pump_casts()
    if upto >= 4:
        out_ffn_phase(0, W_evout, "W_evout")

    if upto >= 5:
      with Scope():
        xt = S.sbuf("xt", [128, KC, TT], F32)
        sq = S.sbuf("sq", [128, KC, TT], BF16)
        hT = S.sbuf("hT", [128, KC, TT], BF16)
        tmpf = [S.sbuf(f"tmpf{i}", [128, TT], F32) for i in range(2)]
        rstd = S.sbuf("rstd", [128, TT], F32)
        stg = [S.sbuf(f"stg{i}", [128, TT], BF16) for i in range(3)]
        wv = [S.sbuf(f"wv{i}", [128, KC, 512], BF16) for i in range(2)]
        wr = WRing("wr", 6, KC)
        pss = S.psum("pss", [128, TT], F32)
        psg = [S.psum(f"psg{i}", [128, TT], F32) for i in range(2)]
        psv = [S.psum(f"psv{i}", [128, TT], F32) for i in range(2)]
        si = 0
        for tt in range(NT):
            tsl = slice(tt * TT, (tt + 1) * TT)
            DMA("sp", xt[:], xT_v[:, :, tsl], [], [xt])
            norm_tile(xt, hT, sq, tmpf, pss, rstd, lambda fc: a1[:, 16 + fc:16 + fc + 1], lambda fc: mv(1, 0, fc))
            for n in range(32):
                wb = wr.load(W_odin, n, "W_odin")
                p = psg[n % 2]
                for kc in range(KC):
                    MM(p[:], wb[:, kc, :], hT[:, kc, :], kc == 0, kc == KC - 1, [wb] + hkeys(hT), [p])
                sb = stg[si % 3]
                si += 1
                if n < 16:
                    EVAC(sb[:], p[:], [p], [sb], scale=float(128 ** -0.5))
                    DMA("pool", q1T[n * 128:(n + 1) * 128, tsl], sb[:], [sb], [])
                else:
                    EVAC(sb[:], p[:], [p], [sb])
                    DMA("pool", k1T[(n - 16) * 128:(n - 15) * 128, tsl], sb[:], [sb], [])
            for cg in range(4):
                w_ = wv[cg % 2]
                for i in range(4):
                    n = 32 + cg * 4 + i
                    DMA("sp", w_[:, :, i * 128:(i + 1) * 128], W_odin[n], wkeys("W_odin", n), [w_])
                for tb in range(4):
                    blk = tt * 4 + tb
                    p = psv[tb % 2]
                    for kc in range(KC):
                        MM(p[:], hT[:, kc, tb * 128:(tb + 1) * 128], w_[:, kc, :], kc == 0, kc == KC - 1, [w_] + hkeys(hT), [p])
                    sb = stg[si % 3]
                    si += 1
                    EVAC(sb[:], p[:], [p], [sb])
                    DMA("pool", v1[blk * 128:(blk + 1) * 128, cg * 512:(cg + 1) * 512], sb[:], [sb], [])

    if upto >= 6:
      with Scope():
        sb01b = S.sbuf("sb01b", [128, 4, TT], BF16)
        sbnegb = S.sbuf("sbnegb", [128, 4, TT], BF16)
        kh = [S.sbuf(f"kh{i}", [128, T], BF16) for i in range(2)]
        vh = [S.sbuf(f"vh{i}", [128, NB, 128], BF16) for i in range(2)]
        qh = [S.sbuf(f"qh{i}", [128, TT], BF16) for i in range(3)]
        ebuf = [S.sbuf(f"ebuf{i}", [128, TT], F32) for i in range(2)]
        spb = [S.sbuf(f"spb{i}", [128, TT], BF16) for i in range(4)]
        ccs = [S.sbuf(f"ccs{i}", [128, TT], F32) for i in range(2)]
        tmpe = [S.sbuf(f"tmpe{i}", [128, TT], F32) for i in range(3)]
        Ab = [S.sbuf(f"Ab{i}", [128, TT], BF16) for i in range(3)]
        ostg = [S.sbuf(f"ostgs{i}", [128, TT], BF16) for i in range(2)]
        Za = [S.psum(f"Za{i}", [128, TT], F32) for i in range(3)]
        Eb = [S.psum(f"Eb{i}", [128, TT], F32) for i in range(2)]
        Cc = S.psum("Cc", [128, TT], F32)
        Oa = [S.psum(f"Oa{i}", [128, TT], F32) for i in range(2)]
        with Scope():
            cf = S.sbuf("cf", [128, 4, TT], F32)
            DMA("sp", cf[:], cst["sb01"].rearrange("p (r t) -> p r t", r=4), [], [cf])
            CP("dve", sb01b[:], cf[:], [cf], [sb01b])
            DMA("sp", cf[:], cst["sbneg"].rearrange("p (r t) -> p r t", r=4), [sb01b], [cf])
            CP("dve", sbnegb[:], cf[:], [cf], [sbnegb])
        items = []
        for h in range(16):
            for qt in range(NT):
                nkb = 4 * qt + 4
                for kb in range(nkb - 1, -1, -1):
                    items.append((h, qt, kb, kb == nkb - 1, kb == 0))
        AHEAD = 2
        state = {}

        def stageA(i):
            h, qt, kb, first, last = items[i]
            if first and qt == 0:
                k_ = kh[h % 2]
                v_ = vh[h % 2]
                DMA("sp", k_[:], k1T[h * 128:(h + 1) * 128, :], [], [k_])
                DMA("sp", v_[:], v1[:, h * 128:(h + 1) * 128].rearrange("(b s) d -> s b d", s=128), [], [v_])
            if first:
                q_ = qh[(h * NT + qt) % 3]
                DMA("sp", q_[:], q1T[h * 128:(h + 1) * 128, qt * TT:(qt + 1) * TT], [], [q_])
            k_ = kh[h % 2]
            q_ = qh[(h * NT + qt) % 3]
            relb_ = kb - 4 * qt
            ksl = slice(kb * 128, (kb + 1) * 128)
            z = Za[i % 3]
            eb = ebuf[i % 2]
            sp = spb[i % 4]
            MM(z[:], k_[:, ksl], q_[:], True, True, [k_, q_], [z])
            ACT(eb[:], z[:], AF.Exp, [z], [eb])
            ACT(sp[:], eb[:], AF.Ln, [eb], [sp], bias=1.0)
            if relb_ >= 0:
                TTo("pool", sp[:], sp[:], sb01b[:, relb_, :], ALU.mult, [sp, sb01b], [sp])

        def stageB1(i):
            h, qt, kb, first, last = items[i]
            k_ = kh[h % 2]
            q_ = qh[(h * NT + qt) % 3]
            relb_ = kb - 4 * qt
            ksl = slice(kb * 128, (kb + 1) * 128)
            sp = spb[i % 4]
            E = Eb[i % 2]
            cc = ccs[i % 2]
            te = tmpe[i % 3]
            MM(E[:], k_[:, ksl], q_[:], True, False, [k_, q_], [E])
            MM(E[:], negUb[:], sp[:], False, relb_ < 0, [negUb, sp], [E])
            if relb_ >= 0:
                MM(E[:], identb[:], sbnegb[:, relb_, :], False, True, [identb, sbnegb], [E])
            if not first:
                CP("dve", cc[:], Cc[:], [Cc], [cc])
                TTo("dve", te[:], E[:], cc[:], ALU.add, [E, cc], [te])
            MM(Cc[:], negonesb[:], sp[:], first, last, [negonesb, sp], [Cc])

        def stageB2(i):
            h, qt, kb, first, last = items[i]
            A = Ab[i % 3]
            if first:
                ACT(A[:], Eb[i % 2][:], AF.Exp, [Eb[i % 2]], [A])
            else:
                ACT(A[:], tmpe[i % 3][:], AF.Exp, [tmpe[i % 3]], [A])

        def stageC(i):
            h, qt, kb, first, last = items[i]
            v_ = vh[h % 2]
            O_ = Oa[(h * NT + qt) % 2]
            og = ostg[(h * NT + qt) % 2]
            A = Ab[i % 3]
            MM(O_[:], v_[:, kb, :], A[:], first, last, [v_, A], [O_])
            if last:
                EVAC(og[:], O_[:], [O_], [og])
                DMA("pool", oT[h * 128:(h + 1) * 128, qt * TT:(qt + 1) * TT], og[:], [og], [])

        n_it = len(items)
        for i in range(n_it + AHEAD + 2):
            if i < n_it:
                stageA(i)
            if 0 <= i - AHEAD < n_it:
                stageB1(i - AHEAD)
            if 0 <= i - AHEAD - 1 < n_it:
                stageB2(i - AHEAD - 1)
            if 0 <= i - AHEAD - 2 < n_it:
                stageC(i - AHEAD - 2)

    if upto >= 7:
        out_ffn_phase(1, W_odout, "W_odout")

    if upto >= 8:
      with Scope():
        xt = S.sbuf("xt", [128, KC, TT], F32)
        sq = S.sbuf("sq", [128, KC, TT], BF16)
        yf = S.sbuf("yf", [128, KC, TT], F32)
        rstd = S.sbuf("rstd", [128, TT], F32)
        otm = [S.sbuf(f"otm{i}", [128, D], F32) for i in range(2)]
        pss = S.psum("pss", [128, TT], F32)
        pso = [S.psum(f"pso{i}", [128, TT], F32) for i in range(2)]
        for tt in range(NT):
            tsl = slice(tt * TT, (tt + 1) * TT)
            DMA("sp", xt[:], xT_v[:, :, tsl], [], [xt])
            norm_tile(xt, None, sq, None, pss, rstd, lambda fc: fing[:, fc:fc + 1], None, outf=yf)
            for tb in range(4):
                blk = tt * 4 + tb
                om = otm[tb % 2]
                for fg in range(4):
                    p = pso[fg % 2]
                    for i in range(4):
                        fc = fg * 4 + i
                        TR(p[:, i * 128:(i + 1) * 128], yf[:, fc, tb * 128:(tb + 1) * 128], ident[:], [(yf.name, fc), ident], [p])
                    EVAC(om[:, fg * 512:(fg + 1) * 512], p[:], [p], [om])
                DMA("pool", out_ap[blk * 128:(blk + 1) * 128, :], om[:], [om], [])
    S.barrier()
    S.finish([])
    return nc


def prep_core_inputs(inputs, b, consts):
    f = lambda a: np.ascontiguousarray(np.asarray(a, dtype=np.float32))
    m = {}
    m["x"] = f(inputs["x"][b])
    m["cT"] = f(np.asarray(inputs["c"][b]).reshape(KC, 128).T)
    m["rel_bias"] = f(inputs["rel_bias"])
    m["ada_w"] = f(inputs["ada_w"])
    m["ada_bT"] = f(np.asarray(inputs["ada_b"]).reshape(2, 96, 128).transpose(2, 0, 1).reshape(128, 192))
    m["gmixT"] = f(np.asarray(inputs["norm_mix_g"]).reshape(2, KC, 128).transpose(2, 0, 1).reshape(128, 32))
    m["gffnT"] = f(np.asarray(inputs["norm_ffn_g"]).reshape(2, KC, 128).transpose(2, 0, 1).reshape(128, 32))
    m["ev_w_in"] = f(inputs["ev_w_in"][0])
    m["kvg_bc"] = f(np.broadcast_to(np.asarray(inputs["ev_kv_norm_g"][0])[None, :], (128, 256)))
    m["ev_w_uk"] = f(inputs["ev_w_uk"][0])
    m["ev_w_uv"] = f(inputs["ev_w_uv"][0])
    m["sinks_bc"] = f(np.broadcast_to(np.asarray(inputs["ev_sinks"][0])[None, :], (128, 16)))
    m["ev_w_out"] = f(inputs["ev_w_out"][0])
    m["od_w_in"] = f(inputs["od_w_in"][0])
    m["od_w_out"] = f(inputs["od_w_out"][0])
    m["ffn_w_up"] = f(inputs["ffn_w_up"])
    m["convwT"] = f(np.asarray(inputs["ffn_conv_w"]).reshape(2, 3, NFC, 128).transpose(3, 0, 1, 2).reshape(128, 2 * 3 * NFC))
    m["convbT"] = f(np.asarray(inputs["ffn_conv_b"]).reshape(2, NFC, 128).transpose(2, 0, 1).reshape(128, 2 * NFC))
    m["ffn_w_down"] = f(inputs["ffn_w_down"])
    m["finalgT"] = f(np.asarray(inputs["final_g"]).reshape(KC, 128).T)
    for k, v in consts.items():
        m["c_" + k] = v
    return m


def kernel(**inputs):
    nc = build_program()
    consts = host_constants()
    shared = None
    in_maps = []
    for core in range(N_CORES):
        b = core % 4
        m = prep_core_inputs(inputs, b, consts)
        if shared is None:
            shared = m
        else:
            for k in m:
                if k not in ("x", "cT"):
                    m[k] = shared[k]
        in_maps.append(m)
    res = run_bass_kernel_spmd(nc, in_maps, core_ids=list(range(N_CORES)))
    out = np.stack([np.asarray(res.results[b]["out"], dtype=np.float32) for b in range(4)], axis=0)
    return out
```
